# Optimizing a Trainium2 kernel written in Bass

```python
import math
import jax, jax.numpy as jnp
from jax import lax
import numpy as np

D_MODEL = 1024
BATCH = 8
SEQ = 8192
DEPTH = 2

CHUNK = 64
MEM_LEN = 256
Q_BLOCK = 128
N_BRANCH = 4
BRANCH_W = D_MODEL // 2
N_IN_PIECES = 12
IN_COLS = N_IN_PIECES * BRANCH_W + N_BRANCH * D_MODEL
SSM_GROUP = 16
SSM_GROUPS = BRANCH_W // SSM_GROUP
SSM_STATE = 64
DT_MIN = 1e-3
DT_MAX = 1e-1
RET_HEADS = 4
RET_DK = BRANCH_W // RET_HEADS
RET_DV = BRANCH_W // RET_HEADS
DIFF_HEADS = 4
DIFF_DH = BRANCH_W // (2 * DIFF_HEADS)
DIFF_DV = 2 * DIFF_DH
MEM_HEADS = 4
MEM_DH = BRANCH_W // MEM_HEADS
ROPE_THETA = 10000.0
EPS = 1e-6
NEG_INF = -1e30

kernel_name = "hybrid_s5_retention_diffattn_memory_trunk"


def rms_norm(x, g=None):
    x32 = x.astype(jnp.float32)
    y = x32 * lax.rsqrt(jnp.mean(x32 * x32, axis=-1, keepdims=True) + EPS)
    if g is not None:
        y = y * g.astype(jnp.float32)
    return y.astype(x.dtype)


def rope_tables(seq, inv_freq):
    pos = jnp.arange(seq, dtype=jnp.float32)
    ang = pos[:, None] * inv_freq[None, :]
    return jnp.cos(ang), jnp.sin(ang)


def rope(x, cos, sin):
    shape = (x.shape[1],) + (1,) * (x.ndim - 3) + (cos.shape[-1],)
    c = cos.reshape(shape)
    s = sin.reshape(shape)
    x1, x2 = jnp.split(x, 2, axis=-1)
    return jnp.concatenate([x1 * c - x2 * s, x2 * c + x1 * s], axis=-1).astype(x.dtype)


def _ssm_combine(left, right):
    ar_l, ai_l, br_l, bi_l = left
    ar_r, ai_r, br_r, bi_r = right
    ar = ar_r * ar_l - ai_r * ai_l
    ai = ar_r * ai_l + ai_r * ar_l
    br = ar_r * br_l - ai_r * bi_l + br_r
    bi = ar_r * bi_l + ai_r * br_l + bi_r
    return (ar, ai, br, bi)


def s5_branch(u, lam_re, lam_im, log_dt, b_re, b_im, c_re, c_im, d_skip, w_glu, b_glu):
    f32 = jnp.float32
    bsz, seq, _ = u.shape
    u32 = u.astype(f32)
    ug = u32.reshape(bsz, seq, SSM_GROUPS, SSM_GROUP)
    lr = jnp.minimum(lam_re.astype(f32), -1e-4)
    li = lam_im.astype(f32)
    dt = jnp.exp(log_dt.astype(f32))[:, None]
    mag = jnp.exp(lr * dt)
    ab_re = mag * jnp.cos(li * dt)
    ab_im = mag * jnp.sin(li * dt)
    nr = ab_re - 1.0
    ni = ab_im
    den = lr * lr + li * li
    f_re = (nr * lr + ni * li) / den
    f_im = (ni * lr - nr * li) / den
    br_ = b_re.astype(f32)
    bi_ = b_im.astype(f32)
    bb_re = f_re[:, :, None] * br_ - f_im[:, :, None] * bi_
    bb_im = f_re[:, :, None] * bi_ + f_im[:, :, None] * br_
    bu_re = jnp.einsum('bsgh,gph->bsgp', ug, bb_re)
    bu_im = jnp.einsum('bsgh,gph->bsgp', ug, bb_im)
    a_re = jnp.broadcast_to(ab_re, bu_re.shape)
    a_im = jnp.broadcast_to(ab_im, bu_im.shape)
    _, _, xr, xi = lax.associative_scan(_ssm_combine, (a_re, a_im, bu_re, bu_im), axis=1)
    y = (jnp.einsum('bsgp,ghp->bsgh', xr, c_re.astype(f32))
         - jnp.einsum('bsgp,ghp->bsgh', xi, c_im.astype(f32)))
    y = y.reshape(bsz, seq, BRANCH_W) + d_skip.astype(f32) * u32
    z = jax.nn.gelu(y)
    out = z * jax.nn.sigmoid(jnp.einsum('bsc,ce->bse', z, w_glu.astype(f32)) + b_glu.astype(f32))
    return out.astype(u.dtype)


def retention_branch(q, k, v, cos, sin):
    f32 = jnp.float32
    bsz, seq = q.shape[0], q.shape[1]
    n_chunks = seq // CHUNK
    qr = rope(q.astype(f32), cos, sin)
    kr = rope(k.astype(f32), cos, sin) * (RET_DK ** -0.5)
    v32 = v.astype(f32)
    log_gamma = jnp.log1p(-(2.0 ** (-5.0 - jnp.arange(RET_HEADS, dtype=f32))))
    idx = jnp.arange(CHUNK, dtype=f32)
    intra = jnp.exp(jnp.abs(idx[:, None] - idx[None, :])[None] * log_gamma[:, None, None])
    q_dec = jnp.exp((idx + 1.0)[None, :] * log_gamma[:, None])
    k_dec = jnp.exp((CHUNK - 1.0 - idx)[None, :] * log_gamma[:, None])
    chunk_dec = jnp.exp(CHUNK * log_gamma)

    def to_chunks(t):
        return t.reshape(bsz, n_chunks, CHUNK, RET_HEADS, t.shape[-1]).transpose(1, 0, 3, 2, 4)

    def step(state, xs):
        qc, kc, vc = xs
        scores = jnp.einsum('bhid,bhjd->bhij', qc, kc) * intra[None]
        o = (jnp.einsum('bhij,bhjv->bhiv', scores, vc)
             + jnp.einsum('bhid,bhdv->bhiv', qc, state) * q_dec[None, :, :, None])
        state = (state * chunk_dec[None, :, None, None]
                 + jnp.einsum('bhjd,bhjv->bhdv', kc * k_dec[None, :, :, None], vc))
        return state, o

    state0 = jnp.zeros((bsz, RET_HEADS, RET_DK, RET_DV), f32)
    _, o = lax.scan(step, state0, (to_chunks(qr), to_chunks(kr), to_chunks(v32)))
    o = o.transpose(1, 0, 3, 2, 4).reshape(bsz, seq, RET_HEADS, RET_DV)
    o = rms_norm(o)
    return o.reshape(bsz, seq, RET_HEADS * RET_DV).astype(q.dtype)


def diff_attn_branch(q, k, v, cos, sin, qn_g, kn_g, lq1, lk1, lq2, lk2, hn_g, lambda_init):
    f32 = jnp.float32
    bsz, seq = q.shape[0], q.shape[1]
    n_blocks = seq // Q_BLOCK
    qr = rope(rms_norm(q, qn_g), cos, sin) * (DIFF_DH ** -0.5)
    kr = rope(rms_norm(k, kn_g), cos, sin)
    lam = (jnp.exp(jnp.sum(lq1.astype(f32) * lk1.astype(f32)))
           - jnp.exp(jnp.sum(lq2.astype(f32) * lk2.astype(f32))) + lambda_init)
    key_chunk = jnp.arange(seq) // CHUNK
    qb = qr.reshape(bsz, n_blocks, Q_BLOCK, DIFF_HEADS, 2, DIFF_DH).transpose(1, 0, 2, 3, 4, 5)

    def attend(args):
        q_blk, blk = args
        q_chunk = (blk * Q_BLOCK + jnp.arange(Q_BLOCK)) // CHUNK
        mask = key_chunk[None, :] <= q_chunk[:, None]
        s = jnp.einsum('bqhmd,bkhmd->bhmqk', q_blk, kr).astype(f32)
        s = jnp.where(mask, s, NEG_INF)
        p = jax.nn.softmax(s, axis=-1)
        a = p[:, :, 0] - lam * p[:, :, 1]
        return jnp.einsum('bhqk,bkhv->bqhv', a.astype(v.dtype), v)

    o = lax.map(attend, (qb, jnp.arange(n_blocks)))
    o = o.transpose(1, 0, 2, 3, 4).reshape(bsz, seq, DIFF_HEADS, DIFF_DV)
    o = rms_norm(o, hn_g) * (1.0 - lambda_init)
    return o.reshape(bsz, seq, DIFF_HEADS * DIFF_DV).astype(q.dtype)


def memory_branch(q, mem_n, w_mem_kv, qn_g, kn_g):
    bsz, seq = q.shape[0], q.shape[1]
    m = mem_n.shape[1]
    kv = jnp.einsum('bmd,dc->bmc', mem_n, w_mem_kv)
    mk, mv = jnp.split(kv, 2, axis=-1)
    mk = rms_norm(mk.reshape(bsz, m, MEM_HEADS, MEM_DH), kn_g)
    mv = mv.reshape(bsz, m, MEM_HEADS, MEM_DH)
    qh = rms_norm(q.reshape(bsz, seq, MEM_HEADS, MEM_DH), qn_g) * (MEM_DH ** -0.5)
    s = jnp.einsum('bshd,bmhd->bhsm', qh, mk).astype(jnp.float32)
    p = jax.nn.softmax(s, axis=-1)
    o = jnp.einsum('bhsm,bmhd->bshd', p.astype(mv.dtype), mv)
    return o.reshape(bsz, seq, BRANCH_W).astype(q.dtype)


def setup_inputs(seed: int = 0) -> dict:
    key = jax.random.key(seed)
    ks = jax.random.split(key, 32)
    f32 = jnp.float32
    nrm = lambda k, shape, scale: jax.random.normal(k, shape, f32) * scale
    gain = lambda k, shape: 1.0 + 0.02 * jax.random.normal(k, shape, f32)
    lam_im_base = math.pi * jnp.arange(SSM_STATE, dtype=f32)
    return {
        "x": nrm(ks[0], (BATCH, SEQ, D_MODEL), 1.0),
        "mem": nrm(ks[1], (BATCH, MEM_LEN, D_MODEL), 1.0),
        "norm_g": gain(ks[2], (DEPTH, D_MODEL)),
        "w_in": nrm(ks[3], (DEPTH, D_MODEL, IN_COLS), D_MODEL ** -0.5),
        "ssm_lambda_re": -0.5 + 0.01 * jax.random.normal(ks[4], (DEPTH, SSM_GROUPS, SSM_STATE), f32),
        "ssm_lambda_im": lam_im_base[None, None, :] + 0.01 * jax.random.normal(ks[5], (DEPTH, SSM_GROUPS, SSM_STATE), f32),
        "ssm_log_dt": jax.random.uniform(ks[6], (DEPTH, SSM_GROUPS), f32, minval=math.log(DT_MIN), maxval=math.log(DT_MAX)),
        "ssm_b_re": nrm(ks[7], (DEPTH, SSM_GROUPS, SSM_STATE, SSM_GROUP), (2 * SSM_GROUP) ** -0.5),
        "ssm_b_im": nrm(ks[8], (DEPTH, SSM_GROUPS, SSM_STATE, SSM_GROUP), (2 * SSM_GROUP) ** -0.5),
        "ssm_c_re": nrm(ks[9], (DEPTH, SSM_GROUPS, SSM_GROUP, SSM_STATE), SSM_STATE ** -0.5),
        "ssm_c_im": nrm(ks[10], (DEPTH, SSM_GROUPS, SSM_GROUP, SSM_STATE), SSM_STATE ** -0.5),
        "ssm_d": nrm(ks[11], (DEPTH, BRANCH_W), 1.0),
        "ssm_w_glu": nrm(ks[12], (DEPTH, BRANCH_W, BRANCH_W), BRANCH_W ** -0.5),
        "ssm_b_glu": nrm(ks[13], (DEPTH, BRANCH_W), 0.01),
        "diff_q_norm_g": gain(ks[14], (DEPTH, DIFF_DH)),
        "diff_k_norm_g": gain(ks[15], (DEPTH, DIFF_DH)),
        "diff_lambda_q1": nrm(ks[16], (DEPTH, DIFF_DH), 0.1),
        "diff_lambda_k1": nrm(ks[17], (DEPTH, DIFF_DH), 0.1),
        "diff_lambda_q2": nrm(ks[18], (DEPTH, DIFF_DH), 0.1),
        "diff_lambda_k2": nrm(ks[19], (DEPTH, DIFF_DH), 0.1),
        "diff_head_norm_g": gain(ks[20], (DEPTH, DIFF_DV)),
        "mem_norm_g": gain(ks[21], (DEPTH, D_MODEL)),
        "w_mem_kv": nrm(ks[22], (DEPTH, D_MODEL, 2 * BRANCH_W), D_MODEL ** -0.5),
        "mem_q_norm_g": gain(ks[23], (DEPTH, MEM_DH)),
        "mem_k_norm_g": gain(ks[24], (DEPTH, MEM_DH)),
        "w_branch": nrm(ks[25], (DEPTH, N_BRANCH, BRANCH_W, D_MODEL), BRANCH_W ** -0.5),
        "b_merge": nrm(ks[26], (DEPTH, N_BRANCH, D_MODEL), 0.01),
        "w_out": nrm(ks[27], (DEPTH, D_MODEL, D_MODEL), D_MODEL ** -0.5),
    }


def reference(x, mem, norm_g, w_in, ssm_lambda_re, ssm_lambda_im, ssm_log_dt, ssm_b_re, ssm_b_im,
              ssm_c_re, ssm_c_im, ssm_d, ssm_w_glu, ssm_b_glu, diff_q_norm_g, diff_k_norm_g,
              diff_lambda_q1, diff_lambda_k1, diff_lambda_q2, diff_lambda_k2, diff_head_norm_g,
              mem_norm_g, w_mem_kv, mem_q_norm_g, mem_k_norm_g, w_branch, b_merge, w_out):
    f32 = jnp.float32
    bsz, seq, _ = x.shape
    diff_inv = ROPE_THETA ** (-jnp.arange(0, DIFF_DH, 2, dtype=f32) / DIFF_DH)
    d_cos, d_sin = rope_tables(seq, diff_inv)
    ret_inv = 1.0 / (ROPE_THETA ** jnp.linspace(0.0, 1.0, RET_DK // 2, dtype=f32))
    r_cos, r_sin = rope_tables(seq, ret_inv)
    split_at = [BRANCH_W * i for i in range(1, N_IN_PIECES + 1)]

    for l in range(DEPTH):
        lambda_init = 0.8 - 0.6 * math.exp(-0.3 * l)
        h = rms_norm(x, norm_g[l])
        proj = jnp.einsum('bsd,dc->bsc', h, w_in[l])
        (ssm_u, ssm_gate, ret_q, ret_k, ret_v, ret_gate, diff_q, diff_k, diff_v, diff_gate,
         mem_q, mem_gate, merge) = jnp.split(proj, split_at, axis=-1)

        a_out = s5_branch(ssm_u, ssm_lambda_re[l], ssm_lambda_im[l], ssm_log_dt[l], ssm_b_re[l],
                          ssm_b_im[l], ssm_c_re[l], ssm_c_im[l], ssm_d[l], ssm_w_glu[l], ssm_b_glu[l])
        a_out = a_out * jax.nn.silu(ssm_gate)

        b_out = retention_branch(ret_q.reshape(bsz, seq, RET_HEADS, RET_DK),
                                 ret_k.reshape(bsz, seq, RET_HEADS, RET_DK),
                                 ret_v.reshape(bsz, seq, RET_HEADS, RET_DV), r_cos, r_sin)
        b_out = b_out * jax.nn.silu(ret_gate)

        c_out = diff_attn_branch(diff_q.reshape(bsz, seq, DIFF_HEADS, 2, DIFF_DH),
                                 diff_k.reshape(bsz, seq, DIFF_HEADS, 2, DIFF_DH),
                                 diff_v.reshape(bsz, seq, DIFF_HEADS, DIFF_DV), d_cos, d_sin,
                                 diff_q_norm_g[l], diff_k_norm_g[l], diff_lambda_q1[l], diff_lambda_k1[l],
                                 diff_lambda_q2[l], diff_lambda_k2[l], diff_head_norm_g[l], lambda_init)
        c_out = c_out * jax.nn.silu(diff_gate)

        mem_n = rms_norm(mem, mem_norm_g[l])
        m_out = memory_branch(mem_q, mem_n, w_mem_kv[l], mem_q_norm_g[l], mem_k_norm_g[l])
        m_out = m_out * jax.nn.silu(mem_gate)

        branches = jnp.stack([a_out, b_out, c_out, m_out], axis=2)
        br = jnp.einsum('bsnc,ncd->bsnd', branches, w_branch[l])
        gates = jax.nn.sigmoid(merge.reshape(bsz, seq, N_BRANCH, D_MODEL) + b_merge[l])
        merged = jnp.sum(gates * br, axis=2)
        x = (x + jnp.einsum('bsd,de->bse', merged, w_out[l])).astype(x.dtype)
    return x
```

```python
import math
import numpy as np
import ml_dtypes
from contextlib import ExitStack
import concourse.bass as bass
import concourse.mybir as mybir
from concourse.bass_utils import run_bass_kernel_spmd

F32 = mybir.dt.float32
BF16 = mybir.dt.bfloat16
AF = mybir.ActivationFunctionType
ALU = mybir.AluOpType
AX = mybir.AxisListType

D = 1024
BW = 512
MEM = 256
EPS = 1e-6
NPIECE_COLS = 10240
P_SSM_U, P_SSM_G, P_RET_Q, P_RET_K, P_RET_V, P_RET_G, P_DQ, P_DK, P_DV, P_DG, P_MQ, P_MG = [i * 512 for i in range(12)]
P_MERGE = 12 * 512

PARAM_NAMES = ["norm_g", "w_in", "ssm_lambda_re", "ssm_lambda_im", "ssm_log_dt", "ssm_b_re", "ssm_b_im",
               "ssm_c_re", "ssm_c_im", "ssm_d", "ssm_w_glu", "ssm_b_glu", "diff_q_norm_g", "diff_k_norm_g",
               "diff_lambda_q1", "diff_lambda_k1", "diff_lambda_q2", "diff_lambda_k2", "diff_head_norm_g",
               "mem_norm_g", "w_mem_kv", "mem_q_norm_g", "mem_k_norm_g", "w_branch", "b_merge", "w_out"]
PARAM_SHAPES = {
    "norm_g": (2, 1024), "w_in": (2, 1024, 10240), "ssm_lambda_re": (2, 32, 64), "ssm_lambda_im": (2, 32, 64),
    "ssm_log_dt": (2, 32), "ssm_b_re": (2, 32, 64, 16), "ssm_b_im": (2, 32, 64, 16), "ssm_c_re": (2, 32, 16, 64),
    "ssm_c_im": (2, 32, 16, 64), "ssm_d": (2, 512), "ssm_w_glu": (2, 512, 512), "ssm_b_glu": (2, 512),
    "diff_q_norm_g": (2, 64), "diff_k_norm_g": (2, 64), "diff_lambda_q1": (2, 64), "diff_lambda_k1": (2, 64),
    "diff_lambda_q2": (2, 64), "diff_lambda_k2": (2, 64), "diff_head_norm_g": (2, 128), "mem_norm_g": (2, 1024),
    "w_mem_kv": (2, 1024, 1024), "mem_q_norm_g": (2, 128), "mem_k_norm_g": (2, 128), "w_branch": (2, 4, 512, 1024),
    "b_merge": (2, 4, 1024), "w_out": (2, 1024, 1024)}


class Buf:
    __slots__ = ("name", "lw", "rd", "sem", "cnt", "kind")

    def __init__(self, name):
        self.name = name
        self.lw = {}
        self.rd = {}
        self.sem = None
        self.cnt = 0


class Eng:
    def __init__(self, name, h, sem, inorder):
        self.name = name
        self.h = h
        self.sem = sem
        self.cnt = 0
        self.waited = {}
        self.inorder = inorder


class K:
    def __init__(self, nc, es):
        self.nc = nc
        self.es = es
        self.sems = {}
        mk = lambda n: es.enter_context(nc.semaphore("sem_" + n))
        self.pe = Eng("pe", nc.tensor, mk("pe"), True)
        self.act = Eng("act", nc.scalar, mk("act"), False)
        self.dve = Eng("dve", nc.vector, mk("dve"), False)
        self.pool = Eng("pool", nc.gpsimd, mk("pool"), False)
        self.sp = Eng("sp", nc.sync, mk("sp"), False)
        self.engs = [self.pe, self.act, self.dve, self.pool, self.sp]
        for e in self.engs:
            self.sems[id(e.sem)] = e.sem
        self.nbuf = 0
        self.dsem = {}
        self.free_slots = []
        self.free_sw = []
        self.active = []

    def buf(self, name=None):
        self.nbuf += 1
        return Buf(name or f"b{self.nbuf}")

    def bufs(self, n, name="b"):
        return [self.buf(f"{name}{i}") for i in range(n)]

    def _wait(self, eng, semid, val):
        if eng.inorder and semid == id(eng.sem):
            return
        if eng.waited.get(semid, 0) >= val:
            return
        eng.h.wait_ge(self.sems[semid], val)
        eng.waited[semid] = val

    def _deps(self, eng, reads, writes):
        for b in reads:
            for s, v in b.lw.items():
                self._wait(eng, s, v)
        for b in writes:
            for s, v in b.lw.items():
                self._wait(eng, s, v)
            for s, v in b.rd.items():
                self._wait(eng, s, v)

    def _commit(self, semid, val, reads, writes):
        for b in reads:
            if b.rd.get(semid, 0) < val:
                b.rd[semid] = val
        for b in writes:
            b.lw = {semid: val}
            b.rd = {}

    def op(self, eng, fn, reads, writes):
        self._deps(eng, reads, writes)
        ins = fn()
        eng.cnt += 1
        ins.then_inc(eng.sem, 1)
        self._commit(id(eng.sem), eng.cnt, reads, writes)
        return ins

    def mm(self, out, pairs, reads, writes, transpose=False):
        eng = self.pe
        self._deps(eng, reads, writes)
        n = len(pairs)
        ins = None
        for i, (l, r) in enumerate(pairs):
            ins = self.nc.tensor.matmul(out, lhsT=l, rhs=r, start=(i == 0), stop=(i == n - 1))
        eng.cnt += 1
        ins.then_inc(eng.sem, 1)
        self._commit(id(eng.sem), eng.cnt, reads, writes)

    def mm_acc(self, out, l, r, start, stop, reads, writes):
        eng = self.pe
        self._deps(eng, reads, writes if start else [])
        ins = self.nc.tensor.matmul(out, lhsT=l, rhs=r, start=start, stop=stop)
        eng.cnt += 1
        ins.then_inc(eng.sem, 1)
        semid = id(eng.sem)
        for b in reads:
            if b.rd.get(semid, 0) < eng.cnt:
                b.rd[semid] = eng.cnt
        for b in writes:
            if start:
                b.lw = {semid: eng.cnt}
                b.rd = {}
            else:
                b.lw[semid] = eng.cnt

    def transposes(self, items, reads, writes):
        eng = self.pe
        self._deps(eng, reads, writes)
        ins = None
        for (o, i, idt) in items:
            ins = self.nc.tensor.transpose(o, i, idt)
        eng.cnt += 1
        ins.then_inc(eng.sem, 1)
        self._commit(id(eng.sem), eng.cnt, reads, writes)

    def dma(self, eng, items, sembuf, reads, writes, slow=False):
        self._deps(eng, reads, writes)
        if sembuf.sem is None:
            fl = self.free_sw if eng is self.pool else self.free_slots
            sembuf.kind = eng is self.pool
            if fl:
                sembuf.sem = fl.pop()
            else:
                sembuf.sem = self.es.enter_context(self.nc.semaphore(f"dsem{len(self.dsem)}"))
                self.dsem[id(sembuf.sem)] = 0
                self.sems[id(sembuf.sem)] = sembuf.sem
            self.active.append(sembuf)
        sid = id(sembuf.sem)
        for (o, i) in items:
            if slow:
                eng.h.dma_start(out=o, in_=i, allow_slow_non_contiguous=True).then_inc(sembuf.sem, 16)
            else:
                eng.h.dma_start(out=o, in_=i).then_inc(sembuf.sem, 16)
            self.dsem[sid] += 16
        self._commit(sid, self.dsem[sid], reads, writes)

    def barrier(self):
        for e in self.engs:
            for o in self.engs:
                if o is not e and o.cnt > 0:
                    self._wait(e, id(o.sem), o.cnt)
            for sid, cnt in self.dsem.items():
                if cnt > 0:
                    self._wait(e, sid, cnt)
        for b in self.active:
            (self.free_sw if b.kind else self.free_slots).append(b.sem)
            b.sem = None
        self.active = []

    def finish(self, out_bufs):
        e = self.sp
        for b in out_bufs:
            for s, v in b.lw.items():
                self._wait(e, s, v)
        self.barrier()


def host_consts(S):
    c = {}
    bf = ml_dtypes.bfloat16
    c["c_ident"] = np.eye(128, dtype=np.float32).astype(bf)
    c["c_identf"] = np.eye(128, dtype=np.float32)
    c["c_ones"] = np.ones((128, 128), np.float32).astype(bf)
    bd = np.zeros((128, 128), np.float32)
    bd[:64, :64] = 1.0
    bd[64:, 64:] = 1.0
    c["c_bd64"] = bd.astype(bf)
    pos = np.arange(S, dtype=np.float32)
    dinv = (10000.0 ** (-np.arange(0, 64, 2, dtype=np.float32) / 64)).astype(np.float32)
    ang = (pos[None, :] * dinv[:, None]).astype(np.float32)
    cosd = np.cos(ang).astype(np.float32)
    sind = np.sin(ang).astype(np.float32)
    c["c_dcos"] = np.tile(cosd, (4, 1)).astype(np.float32)
    c["c_dsin"] = np.tile(sind, (4, 1)).astype(np.float32)
    Rd = np.zeros((128, 128), np.float32)
    for blk in range(2):
        for d in range(32):
            Rd[blk * 64 + d + 32, blk * 64 + d] = -1.0
            Rd[blk * 64 + d, blk * 64 + d + 32] = 1.0
    c["c_rotd"] = Rd.astype(bf)
    rinv = (1.0 / (10000.0 ** np.linspace(0.0, 1.0, 64, dtype=np.float32))).astype(np.float32)
    ang = (pos[None, :] * rinv[:, None]).astype(np.float32)
    c["c_rcos"] = np.tile(np.cos(ang).astype(np.float32), (2, 1))
    c["c_rsin"] = np.tile(np.sin(ang).astype(np.float32), (2, 1))
    Rr = np.zeros((128, 128), np.float32)
    for d in range(64):
        Rr[d + 64, d] = -1.0
        Rr[d, d + 64] = 1.0
    c["c_rotr"] = Rr.astype(bf)
    lg = np.log1p(-(2.0 ** (-5.0 - np.arange(4, dtype=np.float64))))
    i = np.arange(128)
    maskT = np.zeros((4, 128, 128), np.float32)
    for h in range(4):
        m = np.exp(np.abs(i[None, :] - i[:, None]) * lg[h]) * ((i[:, None] // 64) <= (i[None, :] // 64))
        maskT[h] = m
    c["c_rmask"] = np.ascontiguousarray(maskT.transpose(1, 0, 2)).astype(np.float32)
    qdec = np.exp((i[None, :] + 1.0) * lg[:, None])
    c["c_rqdec"] = np.tile(np.tile(qdec, (1, 4))[None], (128, 1, 1)).astype(np.float32)
    kdec = np.exp((127.0 - i[:, None]) * lg[None, :])
    kd_ = np.zeros((128, 128), np.float32)
    kd_[:, :4] = kdec
    c["c_rkdec"] = kd_
    c["c_rcdec"] = np.exp(128.0 * lg).astype(np.float64)
    kk = np.arange(128)
    qq = np.arange(512)
    dm = np.zeros((128, 4, 512), np.float32)
    for j in range(4):
        dm[:, j, :] = ((128 * j + kk[:, None]) // 64 <= qq[None, :] // 64)
    c["c_dmask"] = dm.astype(bf)
    jj = np.arange(128) // 16
    c["c_s5mask"] = (jj[None, :] >= jj[:, None]).astype(np.float32)
    return c


CONST_DT = {"c_ident": BF16, "c_identf": F32, "c_ones": BF16, "c_bd64": BF16, "c_dcos": F32, "c_dsin": F32,
            "c_rotd": BF16, "c_rcos": F32, "c_rsin": F32, "c_rotr": BF16, "c_rmask": F32, "c_rqdec": F32,
            "c_rkdec": F32, "c_s5mask": F32, "c_dmask": BF16}


def build(S, NL, debug=False, phases=("p0", "p1", "mem", "diff", "ret", "s5", "merge")):
    NG = S // 512
    NT = S // 128
    nc = bass.Bass("TRN2", target_bir_lowering=False)
    _ctr = [0]

    def SBT(name, shp, dt):
        _ctr[0] += 1
        return nc.sbuf_tensor(f"{name}_{_ctr[0]}", shp, dt)
    consts = host_consts(S)
    T = {}
    T["x"] = nc.dram_tensor("x", [S, D], F32, kind="ExternalInput").ap()
    T["mem"] = nc.dram_tensor("mem", [MEM, D], F32, kind="ExternalInput").ap()
    for n in PARAM_NAMES:
        T[n] = nc.dram_tensor(n, list(PARAM_SHAPES[n]), F32, kind="ExternalInput").ap()
    for n, v in consts.items():
        if n == "c_rcdec":
            continue
        T[n] = nc.dram_tensor(n, list(v.shape), CONST_DT[n], kind="ExternalInput").ap()
    T["y"] = nc.dram_tensor("y", [S, D], F32, kind="ExternalOutput").ap()
    dk = "ExternalOutput" if debug else "Internal"
    T["hT"] = nc.dram_tensor("hT", [D, S], BF16, kind=dk).ap()
    T["gT"] = nc.dram_tensor("gT", [2048, S], BF16, kind=dk).ap()
    T["brT"] = nc.dram_tensor("brT", [2048, S], BF16, kind=dk).ap()
    T["xmid"] = nc.dram_tensor("xmid", [S, D], F32, kind=dk).ap()

    with ExitStack() as es:
        k = K(nc, es)
        PS = [es.enter_context(nc.psum_tensor(f"ps{i}", [128, 512], F32)) for i in range(8)]
        PSB = k.bufs(8, "ps")
        ident = es.enter_context(SBT("ident", [128, 128], BF16))
        ones = es.enter_context(SBT("ones", [128, 128], BF16))
        epsc = es.enter_context(SBT("epsc", [128, 4], F32))
        identg = es.enter_context(SBT("identg", [128, 128], F32))
        colR = es.enter_context(SBT("colR", [128, 128], F32))
        colR_b = k.buf("colR")
        cb = k.buf("constbuf")
        k.dma(k.sp, [(ident[:], T["c_ident"][:, :]), (ones[:], T["c_ones"][:, :]), (identg[:], T["c_identf"][:, :])], cb, [], [cb])

        def load_cols(dst_ap, n, rows, dst_buf):
            k.op(k.dve, lambda: nc.vector.memset(colR[:], 0.0), [], [colR_b])
            k.dma(k.sp, [(f(colR), src) for (f, src) in rows], colR_b, [], [colR_b])
            k.transposes([(PS[7][:, 0:128], colR[:], identg[:])], [colR_b, cb], [PSB[7]])
            k.op(k.act, lambda: nc.scalar.activation(out=dst_ap, in_=PS[7][:, 0:n], func=AF.Copy), [PSB[7]], [dst_buf])

        k.op(k.dve, lambda: nc.vector.memset(epsc[:, 0:1], EPS), [], [cb])
        k.op(k.dve, lambda: nc.vector.memset(epsc[:, 1:2], 64 * EPS), [], [cb])
        k.op(k.dve, lambda: nc.vector.memset(epsc[:, 2:3], 0.0), [], [cb])
        stg = [es.enter_context(SBT(f"wstg{i}", [128, 2048], F32)) for i in range(2)]
        stg_b = k.bufs(2, "wstg")
        stg_i = [0]

        def load_cast(pieces, dst_buf):
            for (dst, src, a, b) in pieces:
                if a * b > 2048:
                    hb2 = b // 2
                    load_cast([(dst[:, :, 0:hb2], src[:, :, 0:hb2], a, hb2), (dst[:, :, hb2:b], src[:, :, hb2:b], a, b - hb2)], dst_buf)
                    continue
                i = stg_i[0] % 2
                stg_i[0] += 1
                view = stg[i][:, 0:a * b].rearrange("p (a b) -> p a b", a=a)
                k.dma(k.sp, [(view, src)], stg_b[i], [], [stg_b[i]])
                k.op(k.dve, lambda dst=dst, view=view: nc.vector.tensor_copy(out=dst, in_=view), [stg_b[i]], [dst_buf])

        ss_all = [es.enter_context(SBT(f"ss{l}", [128, NT], F32)) for l in range(NL)]
        ss_bufs = k.bufs(NL, "ss")
        k.barrier()

        hT_b = k.bufs(NG, "hT")
        gT_b = k.bufs(NG, "gT")
        brT_b = [k.bufs(NG, f"brT{n}_") for n in range(4)]
        xres_b = [k.bufs(NG, f"xres{l}_") for l in range(NL + 1)]

        def grp(ap2d, g, rows=None):
            return ap2d[:, g * 512:(g + 1) * 512]

        def rstd_from(out_ap, in_ap, scale, bias_col, reads, writes):
            k.op(k.act, lambda: nc.scalar.activation(out=out_ap, in_=in_ap, func=AF.Ln, bias=bias_col, scale=scale), reads, writes)
            k.op(k.act, lambda: nc.scalar.activation(out=out_ap, in_=out_ap, func=AF.Exp, scale=-0.5), writes, writes)

        for l in range(NL):
            xin = T["x"] if l == 0 else T["xmid"]
            xout = T["y"] if l == NL - 1 else T["xmid"]
            w_in = T["w_in"][l]
            wv = lambda c0, n: w_in.rearrange("(kc p) c -> p kc c", p=128)[:, :, c0:c0 + n]

            if "p0" in phases:
                with ExitStack() as ps_:
                    xg = [ps_.enter_context(SBT(f"p0x{i}", [128, 4, D], F32)) for i in range(2)]
                    xg_b = k.bufs(2, "p0x")
                    junk = ps_.enter_context(SBT("p0junk", [128, D], BF16))
                    junk_b = k.buf()
                    ss = ss_all[l]
                    ss_b = ss_bufs[l]
                    import os
                    LV = int(os.environ.get("KLV", "9"))
                    for g in range(NG if LV >= 2 else 0):
                        i = g % 2
                        k.dma(k.sp, [(xg[i][:], xin[g * 512:(g + 1) * 512, :].rearrange("(t p) d -> p t d", p=128))],
                              xg_b[i], [xres_b[l][g]], [xg_b[i]])
                        for t in range(4 if LV >= 3 else 0):
                            col = 4 * g + t
                            k.op(k.act, lambda t=t, col=col, i=i: nc.scalar.activation(
                                out=junk[:], in_=xg[i][:, t, :], func=AF.Square, accum_out=ss[:, col:col + 1]),
                                [xg_b[i]], [junk_b, ss_b])
                    if LV >= 4:
                        rstd_from(ss[:], ss[:], 1.0 / D, epsc[:, 0:1], [ss_b], [ss_b])
                    k.barrier()

            if "p1" in phases:
                ss = ss_all[l]
                ss_b = ss_bufs[l]
                with ExitStack() as ps_:
                    xg = [ps_.enter_context(SBT(f"p1x{i}", [128, 4, D], F32)) for i in range(2)]
                    xg_b = k.bufs(2, "p1x")
                    hb = [ps_.enter_context(SBT(f"p1h{i}", [128, D], BF16)) for i in range(2)]
                    hb_b = k.bufs(2, "p1h")
                    hTs = [ps_.enter_context(SBT(f"p1hT{i}", [128, 8, 512], BF16)) for i in range(2)]
                    hTs_b = k.bufs(2, "p1hT")
                    gs = [ps_.enter_context(SBT(f"p1g{i}", [128, 16, 512], BF16)) for i in range(2)]
                    gs_b = k.bufs(2, "p1g")
                    wg = ps_.enter_context(SBT("p1wg", [128, 8, 2048], BF16))
                    wg_b = k.buf()
                    gbc = ps_.enter_context(SBT("p1gbc", [128, D], F32))
                    gbc_b = k.buf()
                    load_cast([(wg[:, :, i * 512:(i + 1) * 512], wv(c0, 512), 8, 512) for i, c0 in enumerate([P_SSM_G, P_RET_G, P_DG, P_MG])], wg_b)
                    k.dma(k.sp, [(gbc[:], T["norm_g"][l:l + 1, :].partition_broadcast(128))], gbc_b, [], [gbc_b])
                    nps = 0
                    for g in range(NG):
                        i = g % 2
                        k.dma(k.sp, [(xg[i][:], xin[g * 512:(g + 1) * 512, :].rearrange("(t p) d -> p t d", p=128))],
                              xg_b[i], [xres_b[l][g]], [xg_b[i]])
                        for t in range(4):
                            col = 4 * g + t
                            j = t % 2
                            k.op(k.dve, lambda t=t, col=col, i=i, j=j: nc.vector.scalar_tensor_tensor(
                                out=hb[j][:], in0=xg[i][:, t, :], scalar=ss[:, col:col + 1], in1=gbc[:],
                                op0=ALU.mult, op1=ALU.mult), [xg_b[i], ss_b, gbc_b], [hb_b[j]])
                            pb = nps % 2
                            nps += 1
                            pst = PS[pb][:].bitcast(BF16)
                            k.transposes([(pst[:, kc * 128:(kc + 1) * 128], hb[j][:, kc * 128:(kc + 1) * 128], ident[:]) for kc in range(8)],
                                         [hb_b[j], cb], [PSB[pb]])
                            k.op(k.act, lambda t=t, i=i, pst=pst: nc.scalar.activation(
                                out=hTs[i][:, :, t * 128:(t + 1) * 128], in_=pst.rearrange("p (kc s) -> p kc s", kc=8), func=AF.Copy),
                                [PSB[pb]], [hTs_b[i]])
                        k.dma(k.sp, [(T["hT"].rearrange("(kc p) s -> p kc s", p=128)[:, :, g * 512:(g + 1) * 512], hTs[i][:])],
                              hTs_b[i], [hTs_b[i]], [hT_b[g]])
                        for ft in range(16):
                            pb = 2 + ft % 4
                            k.mm(PS[pb][:], [(wg[:, kc, ft * 128:(ft + 1) * 128], hTs[i][:, kc, :]) for kc in range(8)],
                                 [wg_b, hTs_b[i]], [PSB[pb]])
                            k.op(k.act, lambda ft=ft, i=i, pb=pb: nc.scalar.activation(out=gs[i][:, ft, :], in_=PS[pb][:], func=AF.Silu),
                                 [PSB[pb]], [gs_b[i]])
                        k.dma(k.sp, [(T["gT"].rearrange("(ft p) s -> p ft s", p=128)[:, :, g * 512:(g + 1) * 512], gs[i][:])],
                              gs_b[i], [gs_b[i]], [gT_b[g]])
                    k.barrier()

            if "mem" in phases:
                with ExitStack() as ps_:
                    sb = lambda n, shp, dt: ps_.enter_context(SBT(n, shp, dt))
                    wq = sb("mwq", [128, 8, 512], BF16)
                    wkv = sb("mwkv", [128, 8, 1024], BF16)
                    w_b = k.buf()
                    wkv_src = T["w_mem_kv"][l].rearrange("(kc p) c -> p kc c", p=128)
                    load_cast([(wq[:], wv(P_MQ, 512), 8, 512), (wkv[:, :, 0:512], wkv_src[:, :, 0:512], 8, 512),
                               (wkv[:, :, 512:1024], wkv_src[:, :, 512:1024], 8, 512)], w_b)
                    memx = sb("memx", [128, 2, D], F32)
                    memh = sb("memh", [128, D], BF16)
                    mgbc = sb("mgbc", [128, D], F32)
                    mss = sb("mss", [128, 2], F32)
                    mjunk = sb("mjunk", [128, D], BF16)
                    memT = sb("memT", [128, 8, 256], BF16)
                    colg = sb("mcolg", [128, 4], F32)
                    mkT = sb("mkT", [128, 4, 256], BF16)
                    mv = sb("mv", [128, 2, 512], BF16)
                    m_b = k.buf()
                    s_b = k.buf()
                    k.dma(k.sp, [(memx[:], T["mem"].rearrange("(t p) d -> p t d", p=128)),
                                 (mgbc[:], T["mem_norm_g"][l:l + 1, :].partition_broadcast(128))], m_b, [], [m_b])
                    load_cols(colg[:, 0:2], 2, [(lambda R: R[0:1, :], T["mem_q_norm_g"][l:l + 1, :]), (lambda R: R[1:2, :], T["mem_k_norm_g"][l:l + 1, :])], m_b)
                    k.op(k.dve, lambda: nc.vector.tensor_tensor(out=colg[:, 2:3], in0=colg[:, 0:1], in1=colg[:, 1:2], op=ALU.mult), [m_b], [m_b])
                    k.op(k.dve, lambda: nc.vector.tensor_scalar(out=colg[:, 2:3], in0=colg[:, 2:3], scalar1=128.0 ** -0.5, scalar2=None, op0=ALU.mult), [m_b], [m_b])
                    for t in range(2):
                        k.op(k.act, lambda t=t: nc.scalar.activation(out=mjunk[:], in_=memx[:, t, :], func=AF.Square, accum_out=mss[:, t:t + 1]), [m_b], [s_b])
                    rstd_from(mss[:], mss[:], 1.0 / D, epsc[:, 0:1], [s_b], [s_b])
                    for t in range(2):
                        k.op(k.dve, lambda t=t: nc.vector.scalar_tensor_tensor(out=memh[:], in0=memx[:, t, :], scalar=mss[:, t:t + 1], in1=mgbc[:],
                                                                               op0=ALU.mult, op1=ALU.mult), [m_b, s_b], [s_b])
                        pst = PS[0][:].bitcast(BF16)
                        k.transposes([(pst[:, kc * 128:(kc + 1) * 128], memh[:, kc * 128:(kc + 1) * 128], ident[:]) for kc in range(8)], [s_b, cb], [PSB[0]])
                        k.op(k.act, lambda t=t, pst=pst: nc.scalar.activation(out=memT[:, :, t * 128:(t + 1) * 128], in_=pst.rearrange("p (kc s) -> p kc s", kc=8), func=AF.Copy),
                             [PSB[0]], [m_b])
                    sqk = sb("msqk", [128, 256], BF16)
                    rk = sb("mrk", [128, 256], F32)
                    for h in range(4):
                        k.mm(PS[1][:, 0:256], [(wkv[:, kc, h * 128:(h + 1) * 128], memT[:, kc, :]) for kc in range(8)], [w_b, m_b], [PSB[1]])
                        k.op(k.act, lambda: nc.scalar.activation(out=sqk[:], in_=PS[1][:, 0:256], func=AF.Square), [PSB[1]], [s_b])
                        k.mm(PS[2][:, 0:256], [(ones[:], sqk[:])], [s_b, cb], [PSB[2]])
                        rstd_from(rk[:], PS[2][:, 0:256], 1.0 / 128, epsc[:, 0:1], [PSB[2]], [s_b])
                        k.op(k.dve, lambda h=h: nc.vector.scalar_tensor_tensor(out=mkT[:, h, :], in0=PS[1][:, 0:256], scalar=colg[:, 2:3], in1=rk[:],
                                                                               op0=ALU.mult, op1=ALU.mult), [PSB[1], s_b, m_b], [m_b])
                    for mt in range(2):
                        k.mm(PS[3][:], [(memT[:, kc, mt * 128:(mt + 1) * 128], wkv[:, kc, 512:1024]) for kc in range(8)], [w_b, m_b], [PSB[3]])
                        k.op(k.act, lambda mt=mt: nc.scalar.activation(out=mv[:, mt, :], in_=PS[3][:], func=AF.Copy), [PSB[3]], [m_b])
                    hTs = [sb(f"mhT{i}", [128, 8, 512], BF16) for i in range(2)]
                    hTs_b = k.bufs(2)
                    gts = [sb(f"mgt{i}", [128, 4, 512], BF16) for i in range(2)]
                    gts_b = k.bufs(2)
                    osb = [sb(f"mo{i}", [128, 4, 512], BF16) for i in range(2)]
                    osb_b = k.bufs(2)
                    sq = sb("msq", [128, 512], BF16)
                    qs = sb("mqs", [128, 512], BF16)
                    rq = sb("mrq", [128, 512], F32)
                    qn = sb("mqn", [128, 512], BF16)
                    pt = [sb(f"mpt{i}", [128, 512], BF16) for i in range(2)]
                    rl = sb("mrl", [128, 512], F32)
                    of = sb("mof", [128, 512], F32)
                    sq_b, qs_b, rq_b, qn_b, rl_b, of_b = k.bufs(6)
                    pt_b = k.bufs(2)
                    for g in range(NG):
                        i = g % 2
                        k.dma(k.sp, [(hTs[i][:], T["hT"].rearrange("(kc p) s -> p kc s", p=128)[:, :, g * 512:(g + 1) * 512])], hTs_b[i], [hT_b[g]], [hTs_b[i]])
                        k.dma(k.sp, [(gts[i][:], T["gT"][1536:2048, :].rearrange("(h p) s -> p h s", p=128)[:, :, g * 512:(g + 1) * 512])], gts_b[i], [gT_b[g]], [gts_b[i]])
                        for h in range(4):
                            k.mm(PS[0][:], [(wq[:, kc, h * 128:(h + 1) * 128], hTs[i][:, kc, :]) for kc in range(8)], [w_b, hTs_b[i]], [PSB[0]])
                            k.op(k.act, lambda: nc.scalar.activation(out=sq[:], in_=PS[0][:], func=AF.Square), [PSB[0]], [sq_b])
                            k.op(k.act, lambda: nc.scalar.activation(out=qs[:], in_=PS[0][:], func=AF.Copy), [PSB[0]], [qs_b])
                            k.mm(PS[1][:], [(ones[:], sq[:])], [sq_b, cb], [PSB[1]])
                            rstd_from(rq[:], PS[1][:], 1.0 / 128, epsc[:, 0:1], [PSB[1]], [rq_b])
                            k.op(k.dve, lambda: nc.vector.tensor_tensor(out=qn[:], in0=qs[:], in1=rq[:], op=ALU.mult), [qs_b, rq_b], [qn_b])
                            for mt in range(2):
                                k.mm(PS[2 + mt][:], [(mkT[:, h, mt * 128:(mt + 1) * 128], qn[:])], [m_b, qn_b], [PSB[2 + mt]])
                                k.op(k.act, lambda mt=mt: nc.scalar.activation(out=pt[mt][:], in_=PS[2 + mt][:], func=AF.Exp), [PSB[2 + mt]], [pt_b[mt]])
                            k.mm(PS[4][:], [(mv[:, mt, h * 128:(h + 1) * 128], pt[mt][:]) for mt in range(2)], [m_b] + pt_b, [PSB[4]])
                            k.mm(PS[5][:], [(ones[:], pt[mt][:]) for mt in range(2)], [cb] + pt_b, [PSB[5]])
                            k.op(k.dve, lambda: nc.vector.reciprocal(out=rl[:], in_=PS[5][:]), [PSB[5]], [rl_b])
                            k.op(k.dve, lambda: nc.vector.tensor_tensor(out=of[:], in0=PS[4][:], in1=rl[:], op=ALU.mult), [PSB[4], rl_b], [of_b])
                            k.op(k.dve, lambda h=h, i=i: nc.vector.tensor_tensor(out=osb[i][:, h, :], in0=of[:], in1=gts[i][:, h, :], op=ALU.mult), [of_b, gts_b[i]], [osb_b[i]])
                        k.dma(k.sp, [(T["brT"][1536:2048, :].rearrange("(h p) s -> p h s", p=128)[:, :, g * 512:(g + 1) * 512], osb[i][:])], osb_b[i], [osb_b[i]], [brT_b[3][g]])
                    k.barrier()

            if "diff" in phases:
                lambda_init = 0.8 - 0.6 * math.exp(-0.3 * l)
                with ExitStack() as ps_:
                    sb = lambda n, shp, dt: ps_.enter_context(SBT(n, shp, dt))
                    wq = sb("dwq", [128, 8, 512], BF16)
                    wk = sb("dwk", [128, 8, 512], BF16)
                    wvv = sb("dwv", [128, 8, 512], BF16)
                    w_b = k.buf()
                    load_cast([(wq[:], wv(P_DQ, 512), 8, 512), (wk[:], wv(P_DK, 512), 8, 512), (wvv[:], wv(P_DV, 512), 8, 512)], w_b)
                    rot = sb("drot", [128, 128], BF16)
                    bd = sb("dbd", [128, 128], BF16)
                    dmask = sb("dmask", [128, 4, 512], BF16)
                    cols = sb("dcols", [128, 8], F32)
                    L4 = sb("dL4", [128, 4, 64], F32)
                    prod = sb("dprod", [128, 2, 64], F32)
                    s2 = sb("ds2", [128, 2], F32)
                    c_b = k.buf()
                    k.dma(k.sp, [(rot[:], T["c_rotd"][:, :]), (bd[:], T["c_bd64"][:, :]), (dmask[:], T["c_dmask"][:, :, :])]
                          + [(L4[:, i, :], T[nm][l:l + 1, :].partition_broadcast(128)) for i, nm in enumerate(
                              ["diff_lambda_q1", "diff_lambda_k1", "diff_lambda_q2", "diff_lambda_k2"])], c_b, [], [c_b])
                    load_cols(cols[:, 0:3], 3, [(lambda R: R[0:1, 0:64], T["diff_q_norm_g"][l:l + 1, :]), (lambda R: R[0:1, 64:128], T["diff_q_norm_g"][l:l + 1, :]),
                                                (lambda R: R[1:2, 0:64], T["diff_k_norm_g"][l:l + 1, :]), (lambda R: R[1:2, 64:128], T["diff_k_norm_g"][l:l + 1, :]),
                                                (lambda R: R[2:3, :], T["diff_head_norm_g"][l:l + 1, :])], c_b)
                    k.op(k.dve, lambda: nc.vector.tensor_scalar(out=cols[:, 2:3], in0=cols[:, 2:3], scalar1=1.0 - lambda_init, scalar2=None, op0=ALU.mult), [c_b], [c_b])
                    for i in range(2):
                        k.op(k.dve, lambda i=i: nc.vector.tensor_tensor(out=prod[:, i, :], in0=L4[:, 2 * i, :], in1=L4[:, 2 * i + 1, :], op=ALU.mult), [c_b], [c_b])
                        k.op(k.dve, lambda i=i: nc.vector.reduce_sum(out=s2[:, i:i + 1], in_=prod[:, i, :], axis=AX.X), [c_b], [c_b])
                    k.op(k.act, lambda: nc.scalar.activation(out=s2[:], in_=s2[:], func=AF.Exp), [c_b], [c_b])
                    k.op(k.dve, lambda: nc.vector.tensor_tensor(out=cols[:, 3:4], in0=s2[:, 1:2], in1=s2[:, 0:1], op=ALU.subtract), [c_b], [c_b])
                    k.op(k.dve, lambda: nc.vector.tensor_scalar(out=cols[:, 3:4], in0=cols[:, 3:4], scalar1=-lambda_init, scalar2=None, op0=ALU.add), [c_b], [c_b])
                    V_all = sb("dV", [128, NT, 512], BF16)
                    V_b = k.bufs(NG, "dV")
                    KT = sb("dKT", [128, S], BF16)
                    hTs = [sb(f"dhT{i}", [128, 8, 512], BF16) for i in range(2)]
                    hTs_b = k.bufs(2)
                    for g in range(NG):
                        i = g % 2
                        k.dma(k.sp, [(hTs[i][:], T["hT"].rearrange("(kc p) s -> p kc s", p=128)[:, :, g * 512:(g + 1) * 512])], hTs_b[i], [hT_b[g]], [hTs_b[i]])
                        for t in range(4):
                            pb = t % 4
                            k.mm(PS[pb][:], [(hTs[i][:, kc, t * 128:(t + 1) * 128], wvv[:, kc, :]) for kc in range(8)], [w_b, hTs_b[i]], [PSB[pb]])
                            k.op(k.act, lambda g=g, t=t, pb=pb: nc.scalar.activation(out=V_all[:, 4 * g + t, :], in_=PS[pb][:], func=AF.Copy), [PSB[pb]], [V_b[g]])
                    cs = [sb(f"dcs{i}", [128, 2, 512], F32) for i in range(2)]
                    cs_b = k.bufs(2)
                    gts = [sb(f"dgt{i}", [128, 512], BF16) for i in range(2)]
                    gts_b = k.bufs(2)
                    gk = sb("dgk", [128, 512], BF16)
                    sq = sb("dsq", [128, 512], BF16)
                    rs = sb("drs", [128, 512], F32)
                    t1 = sb("dt1", [128, 512], F32)
                    t2 = sb("dt2", [128, 512], F32)
                    qf = sb("dqf", [128, 512], BF16)
                    gk_b, sq_b, rs_b, t1_b, t2_b, qf_b = k.bufs(6)
                    pt = [[sb(f"dpt{m}{j}", [128, 512], BF16) for j in range(2)] for m in range(2)]
                    pt_b = [k.bufs(2) for m in range(2)]
                    A = sb("dA", [128, 512], F32)
                    Bt = sb("dB", [128, 512], F32)
                    r0 = sb("dr0", [128, 512], F32)
                    r1 = sb("dr1", [128, 512], F32)
                    A_b, B_b, r0_b, r1_b = k.bufs(4)
                    ob = [sb(f"dob{i}", [128, 512], BF16) for i in range(2)]
                    ob_b = k.bufs(2)
                    for h in range(4):
                        KT_b = k.bufs(NG, "dKT")
                        for g in range(NG):
                            i = g % 2
                            k.dma(k.sp, [(hTs[i][:], T["hT"].rearrange("(kc p) s -> p kc s", p=128)[:, :, g * 512:(g + 1) * 512])], hTs_b[i], [hT_b[g]], [hTs_b[i]])
                            k.dma(k.sp, [(cs[i][:, 0, :], T["c_dcos"][:, g * 512:(g + 1) * 512]), (cs[i][:, 1, :], T["c_dsin"][:, g * 512:(g + 1) * 512])], cs_b[i], [], [cs_b[i]])
                            k.dma(k.sp, [(gts[i][:], T["gT"][1024 + h * 128:1024 + (h + 1) * 128, g * 512:(g + 1) * 512])], gts_b[i], [gT_b[g]], [gts_b[i]])

                            def qk_prep(w, col, scale, bias_col, out_ap, out_bufs):
                                k.mm(PS[0][:], [(w[:, kc, h * 128:(h + 1) * 128], hTs[i][:, kc, :]) for kc in range(8)], [w_b, hTs_b[i]], [PSB[0]])
                                k.op(k.act, lambda: nc.scalar.activation(out=gk[:], in_=PS[0][:], func=AF.Copy, scale=cols[:, col:col + 1]), [PSB[0], c_b], [gk_b])
                                k.op(k.act, lambda: nc.scalar.activation(out=sq[:], in_=PS[0][:], func=AF.Square, scale=cols[:, col:col + 1]), [PSB[0], c_b], [sq_b])
                                k.mm(PS[1][:], [(bd[:], sq[:])], [sq_b, c_b], [PSB[1]])
                                k.mm(PS[2][:], [(rot[:], gk[:])], [gk_b, c_b], [PSB[2]])
                                rstd_from(rs[:], PS[1][:], scale, bias_col, [PSB[1]], [rs_b])
                                k.op(k.dve, lambda: nc.vector.tensor_tensor(out=t1[:], in0=gk[:], in1=cs[i][:, 0, :], op=ALU.mult), [gk_b, cs_b[i]], [t1_b])
                                k.op(k.dve, lambda: nc.vector.tensor_tensor(out=t2[:], in0=PS[2][:], in1=cs[i][:, 1, :], op=ALU.mult), [PSB[2], cs_b[i]], [t2_b])
                                k.op(k.dve, lambda: nc.vector.tensor_tensor(out=t1[:], in0=t1[:], in1=t2[:], op=ALU.add), [t1_b, t2_b], [t1_b])
                                k.op(k.dve, lambda: nc.vector.tensor_tensor(out=out_ap, in0=t1[:], in1=rs[:], op=ALU.mult), [t1_b, rs_b], out_bufs)

                            qk_prep(wk, 1, 1.0 / 64, epsc[:, 0:1], KT[:, g * 512:(g + 1) * 512], [KT_b[g]])
                            qk_prep(wq, 0, 1.0, epsc[:, 1:2], qf[:], [qf_b])
                            nkb = 4 * (g + 1)
                            for kb in range(nkb):
                                jd = kb - 4 * g
                                par = kb % 2
                                for m in range(2):
                                    pb = 2 * par + m
                                    k.mm(PS[pb][:], [(KT[m * 64:(m + 1) * 64, kb * 128:(kb + 1) * 128], qf[m * 64:(m + 1) * 64, :])], [KT_b[kb // 4], qf_b], [PSB[pb]])
                                    k.op(k.act, lambda m=m, par=par, pb=pb: nc.scalar.activation(out=pt[m][par][:], in_=PS[pb][:], func=AF.Exp), [PSB[pb]], [pt_b[m][par]])
                                    if jd >= 0:
                                        k.op(k.dve, lambda m=m, par=par, jd=jd: nc.vector.tensor_tensor(out=pt[m][par][:], in0=pt[m][par][:], in1=dmask[:, jd, :], op=ALU.mult),
                                             [pt_b[m][par], c_b], [pt_b[m][par]])
                                for m in range(2):
                                    k.mm_acc(PS[4 + m][:], V_all[:, kb, h * 128:(h + 1) * 128], pt[m][par][:], kb == 0, kb == nkb - 1, [V_b[kb // 4], pt_b[m][par]], [PSB[4 + m]])
                                    k.mm_acc(PS[6 + m][:], ones[:], pt[m][par][:], kb == 0, kb == nkb - 1, [cb, pt_b[m][par]], [PSB[6 + m]])
                            k.op(k.dve, lambda: nc.vector.reciprocal(out=r0[:], in_=PS[6][:]), [PSB[6]], [r0_b])
                            k.op(k.dve, lambda: nc.vector.reciprocal(out=r1[:], in_=PS[7][:]), [PSB[7]], [r1_b])
                            k.op(k.dve, lambda: nc.vector.tensor_tensor(out=A[:], in0=PS[4][:], in1=r0[:], op=ALU.mult), [PSB[4], r0_b], [A_b])
                            k.op(k.dve, lambda: nc.vector.tensor_tensor(out=Bt[:], in0=PS[5][:], in1=r1[:], op=ALU.mult), [PSB[5], r1_b], [B_b])
                            k.op(k.dve, lambda: nc.vector.scalar_tensor_tensor(out=A[:], in0=Bt[:], scalar=cols[:, 3:4], in1=A[:], op0=ALU.mult, op1=ALU.add), [A_b, B_b, c_b], [A_b])
                            k.op(k.act, lambda: nc.scalar.activation(out=sq[:], in_=A[:], func=AF.Square), [A_b], [sq_b])
                            k.mm(PS[0][:], [(ones[:], sq[:])], [sq_b, cb], [PSB[0]])
                            rstd_from(rs[:], PS[0][:], 1.0 / 128, epsc[:, 0:1], [PSB[0]], [rs_b])
                            k.op(k.dve, lambda: nc.vector.tensor_tensor(out=A[:], in0=A[:], in1=rs[:], op=ALU.mult), [A_b, rs_b], [A_b])
                            k.op(k.dve, lambda: nc.vector.scalar_tensor_tensor(out=ob[i][:], in0=A[:], scalar=cols[:, 2:3], in1=gts[i][:], op0=ALU.mult, op1=ALU.mult),
                                 [A_b, c_b, gts_b[i]], [ob_b[i]])
                            k.dma(k.sp, [(T["brT"][1024 + h * 128:1024 + (h + 1) * 128, g * 512:(g + 1) * 512], ob[i][:])], ob_b[i], [ob_b[i]], [brT_b[2][g]])
                    k.barrier()

            if "ret" in phases:
                with ExitStack() as ps_:
                    sb = lambda n, shp, dt: ps_.enter_context(SBT(n, shp, dt))
                    wq = sb("rwq", [128, 8, 512], BF16)
                    wk = sb("rwk", [128, 8, 512], BF16)
                    wvv = sb("rwv", [128, 8, 512], BF16)
                    w_b = k.buf()
                    load_cast([(wq[:], wv(P_RET_Q, 512), 8, 512), (wk[:], wv(P_RET_K, 512), 8, 512), (wvv[:], wv(P_RET_V, 512), 8, 512)], w_b)
                    rot = sb("rrot", [128, 128], BF16)
                    rmask = sb("rmask", [128, 4, 128], F32)
                    rqdec = sb("rqdec", [128, 4, 512], F32)
                    rkdec = sb("rkdec", [128, 128], F32)
                    c_b = k.buf()
                    k.dma(k.sp, [(rot[:], T["c_rotr"][:, :]), (rmask[:], T["c_rmask"][:, :, :]), (rqdec[:], T["c_rqdec"][:, :, :]),
                                 (rkdec[:], T["c_rkdec"][:, :])], c_b, [], [c_b])
                    V_all = sb("rV", [128, NT, 512], BF16)
                    V_b = k.bufs(NG, "rV")
                    hTs = [sb(f"rhT{i}", [128, 8, 512], BF16) for i in range(2)]
                    hTs_b = k.bufs(2)
                    for g in range(NG):
                        i = g % 2
                        k.dma(k.sp, [(hTs[i][:], T["hT"].rearrange("(kc p) s -> p kc s", p=128)[:, :, g * 512:(g + 1) * 512])], hTs_b[i], [hT_b[g]], [hTs_b[i]])
                        for t in range(4):
                            pb = t % 4
                            k.mm(PS[pb][:], [(hTs[i][:, kc, t * 128:(t + 1) * 128], wvv[:, kc, :]) for kc in range(8)], [w_b, hTs_b[i]], [PSB[pb]])
                            k.op(k.act, lambda g=g, t=t, pb=pb: nc.scalar.activation(out=V_all[:, 4 * g + t, :], in_=PS[pb][:], func=AF.Copy), [PSB[pb]], [V_b[g]])
                    cs = [sb(f"rcs{i}", [128, 2, 512], F32) for i in range(2)]
                    cs_b = k.bufs(2)
                    gts = [sb(f"rgt{i}", [128, 512], BF16) for i in range(2)]
                    gts_b = k.bufs(2)
                    qs = sb("rqs", [128, 512], BF16)
                    t1 = sb("rt1", [128, 512], F32)
                    t2 = sb("rt2", [128, 512], F32)
                    qr = sb("rqr", [128, 512], BF16)
                    qd = sb("rqd", [128, 512], BF16)
                    kr = sb("rkr", [128, 512], BF16)
                    qs_b, t1_b, t2_b, qr_b, qd_b, kr_b = k.bufs(6)
                    kd = [sb(f"rkd{i}", [128, 128], BF16) for i in range(2)]
                    kd_b = k.bufs(2)
                    sm = [sb(f"rsm{i}", [128, 128], BF16) for i in range(2)]
                    sm_b = k.bufs(2)
                    st = [sb(f"rst{i}", [128, 128], F32) for i in range(2)]
                    st_b = k.bufs(2)
                    stb = [sb(f"rstb{i}", [128, 128], BF16) for i in range(2)]
                    stb_b = k.bufs(2)
                    oT = sb("roT", [128, 512], F32)
                    oT_b = k.buf()
                    sq = sb("rsq", [128, 512], BF16)
                    rs = sb("rrs", [128, 512], F32)
                    sq_b, rs_b = k.bufs(2)
                    ob = [sb(f"rob{i}", [128, 512], BF16) for i in range(2)]
                    ob_b = k.bufs(2)
                    cdec = consts["c_rcdec"]
                    for h in range(4):
                        hc = slice(h * 128, (h + 1) * 128)
                        k.op(k.dve, lambda: nc.vector.memset(st[0][:], 0.0), [], [st_b[0]])
                        k.op(k.dve, lambda: nc.vector.memset(stb[0][:], 0.0), [], [stb_b[0]])
                        cur = 0
                        for g in range(NG):
                            i = g % 2
                            k.dma(k.sp, [(hTs[i][:], T["hT"].rearrange("(kc p) s -> p kc s", p=128)[:, :, g * 512:(g + 1) * 512])], hTs_b[i], [hT_b[g]], [hTs_b[i]])
                            k.dma(k.sp, [(cs[i][:, 0, :], T["c_rcos"][:, g * 512:(g + 1) * 512]), (cs[i][:, 1, :], T["c_rsin"][:, g * 512:(g + 1) * 512])], cs_b[i], [], [cs_b[i]])
                            k.dma(k.sp, [(gts[i][:], T["gT"][512 + h * 128:512 + (h + 1) * 128, g * 512:(g + 1) * 512])], gts_b[i], [gT_b[g]], [gts_b[i]])

                            def rope_fm(w, scale):
                                k.mm(PS[0][:], [(w[:, kc, hc], hTs[i][:, kc, :]) for kc in range(8)], [w_b, hTs_b[i]], [PSB[0]])
                                k.op(k.act, lambda: nc.scalar.activation(out=qs[:], in_=PS[0][:], func=AF.Copy, scale=scale), [PSB[0]], [qs_b])
                                k.mm(PS[2][:], [(rot[:], qs[:])], [qs_b, c_b], [PSB[2]])
                                k.op(k.dve, lambda: nc.vector.tensor_tensor(out=t1[:], in0=qs[:], in1=cs[i][:, 0, :], op=ALU.mult), [qs_b, cs_b[i]], [t1_b])
                                k.op(k.dve, lambda: nc.vector.tensor_tensor(out=t2[:], in0=PS[2][:], in1=cs[i][:, 1, :], op=ALU.mult), [PSB[2], cs_b[i]], [t2_b])
                                k.op(k.dve, lambda: nc.vector.tensor_tensor(out=t1[:], in0=t1[:], in1=t2[:], op=ALU.add), [t1_b, t2_b], [t1_b])

                            rope_fm(wq, 1.0)
                            k.op(k.act, lambda: nc.scalar.activation(out=qr[:], in_=t1[:], func=AF.Copy), [t1_b], [qr_b])
                            k.op(k.dve, lambda: nc.vector.tensor_tensor(out=qd[:], in0=t1[:], in1=rqdec[:, h, :], op=ALU.mult), [t1_b, c_b], [qd_b])
                            rope_fm(wk, 128.0 ** -0.5)
                            k.op(k.act, lambda: nc.scalar.activation(out=kr[:], in_=t1[:], func=AF.Copy), [t1_b], [kr_b])
                            for t in range(4):
                                blk = slice(t * 128, (t + 1) * 128)
                                tt = 4 * g + t
                                j2 = t % 2
                                nxt = 1 - cur
                                pst = PS[4][:].bitcast(BF16)
                                k.transposes([(pst[:, 0:128], kr[:, blk], ident[:])], [kr_b, cb], [PSB[4]])
                                k.op(k.act, lambda j2=j2, pst=pst: nc.scalar.activation(out=kd[j2][:], in_=pst[:, 0:128], func=AF.Copy, scale=rkdec[:, h:h + 1]),
                                     [PSB[4], c_b], [kd_b[j2]])
                                k.mm(PS[1][:, 0:128], [(kr[:, blk], qr[:, blk])], [kr_b, qr_b], [PSB[1]])
                                k.op(k.dve, lambda j2=j2: nc.vector.tensor_tensor(out=sm[j2][:], in0=PS[1][:, 0:128], in1=rmask[:, h, :], op=ALU.mult), [PSB[1], c_b], [sm_b[j2]])
                                k.mm(PS[3][:, 0:128], [(V_all[:, tt, hc], sm[j2][:]), (stb[cur][:], qd[:, blk])], [V_b[g], sm_b[j2], stb_b[cur], qd_b], [PSB[3]])
                                k.op(k.act, lambda blk=blk: nc.scalar.activation(out=oT[:, blk], in_=PS[3][:, 0:128], func=AF.Copy), [PSB[3]], [oT_b])
                                k.mm(PS[5][:, 0:128], [(kd[j2][:], V_all[:, tt, hc])], [kd_b[j2], V_b[g]], [PSB[5]])
                                k.op(k.dve, lambda cur=cur, nxt=nxt: nc.vector.scalar_tensor_tensor(out=st[nxt][:], in0=st[cur][:], scalar=float(cdec[h]), in1=PS[5][:, 0:128],
                                                                                                  op0=ALU.mult, op1=ALU.add), [st_b[cur], PSB[5]], [st_b[nxt]])
                                k.op(k.act, lambda nxt=nxt: nc.scalar.activation(out=stb[nxt][:], in_=st[nxt][:], func=AF.Copy), [st_b[nxt]], [stb_b[nxt]])
                                cur = nxt
                            k.op(k.act, lambda: nc.scalar.activation(out=sq[:], in_=oT[:], func=AF.Square), [oT_b], [sq_b])
                            k.mm(PS[0][:], [(ones[:], sq[:])], [sq_b, cb], [PSB[0]])
                            rstd_from(rs[:], PS[0][:], 1.0 / 128, epsc[:, 0:1], [PSB[0]], [rs_b])
                            k.op(k.dve, lambda: nc.vector.tensor_tensor(out=rs[:], in0=oT[:], in1=rs[:], op=ALU.mult), [oT_b, rs_b], [rs_b])
                            k.op(k.dve, lambda: nc.vector.tensor_tensor(out=ob[i][:], in0=rs[:], in1=gts[i][:], op=ALU.mult), [rs_b, gts_b[i]], [ob_b[i]])
                            k.dma(k.sp, [(T["brT"][512 + h * 128:512 + (h + 1) * 128, g * 512:(g + 1) * 512], ob[i][:])], ob_b[i], [ob_b[i]], [brT_b[1][g]])
                    k.barrier()

            if "s5" in phases:
                NSC = S // 64
                with ExitStack() as ps_:
                    sb = lambda n, shp, dt: ps_.enter_context(SBT(n, shp, dt))
                    pA_ = ExitStack()
                    pT_ = ExitStack()
                    sbA = lambda n, shp, dt: pA_.enter_context(SBT(n, shp, dt))
                    sbT = lambda n, shp, dt: pT_.enter_context(SBT(n, shp, dt))
                    V = nc.vector
                    G = 32
                    bglu = sb("s5bglu", [128, 4], F32)
                    U_all = sb("s5U", [128, G, 8, NSC], BF16)
                    Xb = sb("s5Xb", [128, G, NSC], BF16)
                    C1 = sbA("s5C1", [128, G, 16], F32)
                    C2 = sbA("s5C2", [128, G, 16], F32)
                    DCOL = sbA("s5DCOL", [128, G], F32)
                    identf = sbA("s5idf", [128, 128], F32)
                    s5mask = sbA("s5mask", [128, 128], F32)
                    N1 = sbA("s5N1", [128, G, 16], F32)
                    N2 = sbA("s5N2", [128, G, 16], F32)
                    AFr = sbA("s5AFr", [128, G, 65], F32)
                    AFi = sbA("s5AFi", [128, G, 65], F32)
                    ABr = sbA("s5ABr", [128, G, 64], F32)
                    ABi = sbA("s5ABi", [128, G, 64], F32)
                    ANr = sbA("s5ANr", [128, G, 8], F32)
                    ANi = sbA("s5ANi", [128, G, 8], F32)
                    c1 = sbA("s5c1", [128, G], F32)
                    c2 = sbA("s5c2", [128, G], F32)
                    pp_b = k.buf("s5pp")
                    dv = lambda fn, rd=(), wr=(): k.op(k.dve, fn, [pp_b] + list(rd), [pp_b] + list(wr))
                    tt = lambda o, a, b, op: dv(lambda: V.tensor_tensor(out=o, in0=a, in1=b, op=op))
                    ts = lambda o, a, s1, op0, s2=None, op1=None: dv(lambda: (V.tensor_scalar(out=o, in0=a, scalar1=s1, scalar2=s2, op0=op0, op1=op1)
                                                                               if op1 is not None else V.tensor_scalar(out=o, in0=a, scalar1=s1, scalar2=None, op0=op0)))
                    LR = sbT("s5LR", [128, G], F32)
                    LI = sbT("s5LI", [128, G], F32)
                    DT = sbT("s5DT", [128, G], F32)
                    R1 = sbT("s5R1", [128, G, 16], F32)
                    R2 = sbT("s5R2", [128, G, 16], F32)
                    CN = sbT("s5CN", [128, 8, 128], F32)
                    ld_b = k.buf()
                    lre, lim = T["ssm_lambda_re"][l], T["ssm_lambda_im"][l]
                    load_cols(LR[:], 32, [(lambda R: R[0:32, 0:64], lre), (lambda R: R[0:32, 64:128], lre)], pp_b)
                    load_cols(LI[:], 32, [(lambda R: R[0:32, 0:64], lim), (lambda R: R[0:32, 64:128], lim)], pp_b)
                    load_cols(bglu[:], 4, [(lambda R: R[0:4, :], T["ssm_b_glu"][l].rearrange("(c p) -> c p", p=128))], pp_b)
                    load_cols(DCOL[:], 32, [(lambda R, j=j: R[0:32, j * 16:(j + 1) * 16], T["ssm_d"][l].rearrange("(g h) -> g h", h=16)) for j in range(8)], pp_b)
                    bre, bim = T["ssm_b_re"][l], T["ssm_b_im"][l]
                    cre, cim = T["ssm_c_re"][l], T["ssm_c_im"][l]
                    k.dma(k.sp, [(DT[:], T["ssm_log_dt"][l:l + 1, :].partition_broadcast(128)),
                                 ] + [(dst_[lo:lo + 64, c8 * 8:(c8 + 1) * 8], src_[c8 * 8:(c8 + 1) * 8].rearrange("g p h -> p g h"))
                                      for (dst_, lo, src_) in [(R1, 0, bre), (R1, 64, bim), (R2, 0, bim), (R2, 64, bre)] for c8 in range(4)] + [
                                 (identf[:], T["c_identf"][:, :]), (s5mask[:], T["c_s5mask"][:, :])]
                          + [(CN[:, ri * 4 + ct, dup * 64:(dup + 1) * 64], src[ct * 8:(ct + 1) * 8].rearrange("g h p -> (g h) p"))
                             for ri, src in enumerate([cre, cim]) for ct in range(4) for dup in range(2)],
                          ld_b, [], [ld_b, pp_b])
                    CT = sbT("s5CT", [128, 2, G, 16], F32)
                    for ri in range(2):
                        for ct in range(4):
                            k.transposes([(PS[0][:, 0:128], CN[:, ri * 4 + ct, :], identf[:])], [pp_b], [PSB[0]])
                            k.op(k.act, lambda ri=ri, ct=ct: nc.scalar.activation(out=CT[:, ri, ct * 8:(ct + 1) * 8, :], in_=PS[0][:, 0:128].rearrange("p (g h) -> p g h", h=16), func=AF.Copy),
                                 [PSB[0]], [pp_b])
                    dv(lambda: V.tensor_copy(out=C1[0:64], in_=CT[0:64, 0]))
                    ts(C1[64:128], CT[64:128, 1], -1.0, ALU.mult)
                    ts(C2[0:64], CT[0:64, 1], -1.0, ALU.mult)
                    ts(C2[64:128], CT[64:128, 0], -1.0, ALU.mult)
                    sm_ = lambda n: sbT("s5_" + n, [128, G], F32)
                    lr, mag, th, x2, sn, cs_, ar, ai, tA, tB, fre, fim, FS, FR2, nfim = [sm_(n) for n in
                        ["lr", "mag", "th", "x2", "sn", "cs", "ar", "ai", "tA", "tB", "fre", "fim", "FS", "FR2", "nfim"]]
                    k.op(k.act, lambda: nc.scalar.activation(out=DT[:], in_=DT[:], func=AF.Exp), [pp_b], [pp_b])
                    ts(lr[:], LR[:], -1e-4, ALU.min)
                    tt(tA[:], lr[:], DT[:], ALU.mult)
                    k.op(k.act, lambda: nc.scalar.activation(out=mag[:], in_=tA[:], func=AF.Exp), [pp_b], [pp_b])
                    tt(th[:], LI[:], DT[:], ALU.mult)
                    NDBL = 6
                    ts(th[:], th[:], 1.0 / (2 ** NDBL), ALU.mult)
                    tt(x2[:], th[:], th[:], ALU.mult)
                    ts(sn[:], x2[:], -1.0 / 72, ALU.mult, 1.0, ALU.add)
                    for c_ in (42.0, 20.0, 6.0):
                        tt(sn[:], sn[:], x2[:], ALU.mult)
                        ts(sn[:], sn[:], -1.0 / c_, ALU.mult, 1.0, ALU.add)
                    tt(sn[:], sn[:], th[:], ALU.mult)
                    ts(cs_[:], x2[:], -1.0 / 90, ALU.mult, 1.0, ALU.add)
                    for c_ in (56.0, 30.0, 12.0, 2.0):
                        tt(cs_[:], cs_[:], x2[:], ALU.mult)
                        ts(cs_[:], cs_[:], -1.0 / c_, ALU.mult, 1.0, ALU.add)
                    for _ in range(NDBL):
                        tt(tA[:], cs_[:], cs_[:], ALU.mult)
                        tt(tB[:], sn[:], sn[:], ALU.mult)
                        tt(sn[:], sn[:], cs_[:], ALU.mult)
                        ts(sn[:], sn[:], 2.0, ALU.mult)
                        tt(cs_[:], tA[:], tB[:], ALU.subtract)
                    tt(ar[:], mag[:], cs_[:], ALU.mult)
                    tt(ai[:], mag[:], sn[:], ALU.mult)
                    ts(tA[:], ar[:], -1.0, ALU.add)
                    tt(tB[:], lr[:], lr[:], ALU.mult)
                    tt(x2[:], LI[:], LI[:], ALU.mult)
                    tt(tB[:], tB[:], x2[:], ALU.add)
                    dv(lambda: V.reciprocal(out=tB[:], in_=tB[:]))
                    tt(fre[:], tA[:], lr[:], ALU.mult)
                    tt(x2[:], ai[:], LI[:], ALU.mult)
                    tt(fre[:], fre[:], x2[:], ALU.add)
                    tt(fre[:], fre[:], tB[:], ALU.mult)
                    tt(fim[:], ai[:], lr[:], ALU.mult)
                    tt(x2[:], tA[:], LI[:], ALU.mult)
                    tt(fim[:], fim[:], x2[:], ALU.subtract)
                    tt(fim[:], fim[:], tB[:], ALU.mult)
                    ts(nfim[:], fim[:], -1.0, ALU.mult)
                    ts(FS[0:64], fim[0:64], -1.0, ALU.mult)
                    dv(lambda: V.tensor_copy(out=FS[64:128], in_=fim[64:128]))
                    ts(FR2[0:64], fre[0:64], -1.0, ALU.mult)
                    dv(lambda: V.tensor_copy(out=FR2[64:128], in_=fre[64:128]))
                    tmp3 = sbT("s5tmp3", [128, G, 16], F32)
                    bc = lambda a: a[:].unsqueeze(2).to_broadcast([128, G, 16])
                    tt(N1[:], R1[:], bc(fre), ALU.mult)
                    tt(tmp3[:], R2[:], bc(FS), ALU.mult)
                    tt(N1[:], N1[:], tmp3[:], ALU.add)
                    tt(N2[:], R2[:], bc(FR2), ALU.mult)
                    tt(tmp3[:], R1[:], bc(nfim), ALU.mult)
                    tt(N2[:], N2[:], tmp3[:], ALU.add)
                    pw1 = sbT("s5pw1", [128, G, 32], F32)
                    pw2 = sbT("s5pw2", [128, G, 32], F32)
                    Pr, Pi = sm_("Pr"), sm_("Pi")

                    def cmul_block(dr, di, sr, si, n):
                        bp = lambda a: a[:].unsqueeze(2).to_broadcast([128, G, n])
                        tt(pw1[:, :, 0:n], sr, bp(Pr), ALU.mult)
                        tt(pw2[:, :, 0:n], si, bp(Pi), ALU.mult)
                        tt(dr, pw1[:, :, 0:n], pw2[:, :, 0:n], ALU.subtract)
                        tt(pw1[:, :, 0:n], sr, bp(Pi), ALU.mult)
                        tt(pw2[:, :, 0:n], si, bp(Pr), ALU.mult)
                        tt(di, pw1[:, :, 0:n], pw2[:, :, 0:n], ALU.add)

                    def psquare():
                        tt(tA[:], Pr[:], Pr[:], ALU.mult)
                        tt(tB[:], Pi[:], Pi[:], ALU.mult)
                        tt(Pi[:], Pi[:], Pr[:], ALU.mult)
                        ts(Pi[:], Pi[:], 2.0, ALU.mult)
                        tt(Pr[:], tA[:], tB[:], ALU.subtract)

                    dv(lambda: V.memset(AFr[:, :, 0:1], 1.0))
                    dv(lambda: V.memset(AFi[:, :, 0:1], 0.0))
                    dv(lambda: V.memset(ABr[:, :, 63:64], 1.0))
                    dv(lambda: V.memset(ABi[:, :, 63:64], 0.0))
                    dv(lambda: V.tensor_copy(out=Pr[:], in_=ar[:]))
                    dv(lambda: V.tensor_copy(out=Pi[:], in_=ai[:]))
                    E = 1
                    while E <= 32:
                        cmul_block(AFr[:, :, E:2 * E], AFi[:, :, E:2 * E], AFr[:, :, 0:E], AFi[:, :, 0:E], E)
                        cmul_block(ABr[:, :, 64 - 2 * E:64 - E], ABi[:, :, 64 - 2 * E:64 - E], ABr[:, :, 64 - E:64], ABi[:, :, 64 - E:64], E)
                        psquare()
                        E *= 2
                    dv(lambda: V.tensor_copy(out=AFr[:, :, 64:65], in_=Pr[:].unsqueeze(2)))
                    dv(lambda: V.tensor_copy(out=AFi[:, :, 64:65], in_=Pi[:].unsqueeze(2)))
                    dv(lambda: V.tensor_copy(out=c1[:], in_=Pr[:]))
                    ts(c2[0:64], Pi[0:64], -1.0, ALU.mult)
                    dv(lambda: V.tensor_copy(out=c2[64:128], in_=Pi[64:128]))
                    tt(tA[:], ar[:], ar[:], ALU.mult)
                    tt(tB[:], ai[:], ai[:], ALU.mult)
                    tt(tA[:], tA[:], tB[:], ALU.add)
                    dv(lambda: V.reciprocal(out=tA[:], in_=tA[:]))
                    tt(Pr[:], ar[:], tA[:], ALU.mult)
                    tt(Pi[:], ai[:], tA[:], ALU.mult)
                    ts(Pi[:], Pi[:], -1.0, ALU.mult)
                    dv(lambda: V.memset(ANr[:, :, 0:1], 1.0))
                    dv(lambda: V.memset(ANi[:, :, 0:1], 0.0))
                    E = 1
                    while E <= 4:
                        cmul_block(ANr[:, :, E:2 * E], ANi[:, :, E:2 * E], ANr[:, :, 0:E], ANi[:, :, 0:E], E)
                        psquare()
                        E *= 2

                    k.barrier()
                    pT_.close()
                    U_b = k.bufs(G, "s5U")
                    Sp_b = k.buf()
                    Xb_b = k.buf()
                    s5T_d = nc.dram_tensor(f"s5Tdram{l}", [G, 128, 1024], BF16).ap()
                    s5V_d = nc.dram_tensor(f"s5Vdram{l}", [G, 128, 1040], BF16).ap()
                    TV_b = k.bufs(G, "s5TV")
                    with ExitStack() as p1_:
                        sb1 = lambda n, shp, dt: p1_.enter_context(SBT(n, shp, dt))
                        wu = sb1("s5wu", [128, 8, 512], BF16)
                        wu_b = k.buf()
                        load_cast([(wu[:], wv(P_SSM_U, 512), 8, 512)], wu_b)
                        UT = [sb1(f"s5UT{i}", [128, 8, 8, NSC], BF16) for i in range(2)]
                        UT_b = k.bufs(2)
                        hTs = [sb1(f"s5hT{i}", [128, 8, 512], BF16) for i in range(2)]
                        hTs_b = k.bufs(2)
                        for ct in range(4):
                            ui = ct % 2
                            for g in range(NG):
                                i = (ct * NG + g) % 2
                                k.dma(k.sp, [(hTs[i][:], T["hT"].rearrange("(kc p) s -> p kc s", p=128)[:, :, g * 512:(g + 1) * 512])], hTs_b[i], [hT_b[g]], [hTs_b[i]])
                                pb = g % 4
                                k.mm(PS[pb][:], [(wu[:, kc, ct * 128:(ct + 1) * 128], hTs[i][:, kc, :]) for kc in range(8)], [wu_b, hTs_b[i]], [PSB[pb]])
                                k.op(k.act, lambda g=g, pb=pb, ui=ui: nc.scalar.activation(out=UT[ui][:, :, :, g * 8:(g + 1) * 8],
                                                                                         in_=PS[pb][:].rearrange("p (sc jb j) -> p j jb sc", jb=8, j=8), func=AF.Copy),
                                     [PSB[pb]], [UT_b[ui]])
                            k.dma(k.sp, [(U_all[j * 16:(j + 1) * 16, ct * 8 + gl], UT[ui][gl * 16:(gl + 1) * 16, j]) for gl in range(8) for j in range(8)],
                                  UT_b[ui], [UT_b[ui]], [U_b[ct * 8 + gl] for gl in range(8)])
                        k.barrier()
                    with ExitStack() as p1_:
                        sb1 = lambda n, shp, dt: p1_.enter_context(SBT(n, shp, dt))
                        Spri = sb1("s5Sp", [128, G, NSC], F32)
                        Ssec = sb1("s5Ss", [128, G, NSC], F32)
                        WT = [sb1(f"s5WT{i}", [128, 64, 16], BF16) for i in range(2)]
                        VX = [sb1(f"s5VX{i}", [128, 65, 16], BF16) for i in range(2)]
                        L0 = [sb1(f"s5L0{i}", [128, 8, 16], BF16) for i in range(2)]
                        g1 = sb1("s5g1", [128, 65, 16], F32)
                        g2 = sb1("s5g2", [128, 65, 16], F32)
                        Wsb = [sb1(f"s5W{i}", [128, 8, 128], BF16) for i in range(2)]
                        Wsw = [sb1(f"s5Wsw{i}", [128, 8, 128], BF16) for i in range(2)]
                        Tsb = [sb1(f"s5T{i}", [128, 8, 128], BF16) for i in range(2)]
                        tmpT = sb1("s5tmpT", [128, 128], F32)
                        WT_b, VX_b, L0_b, W_b2, T_b2 = k.bufs(2), k.bufs(2), k.bufs(2), k.bufs(2), k.bufs(2)
                        g_b = k.buf()
                        for gg in range(G):
                            i = gg % 2

                            def gen(out3, n, Ar, Ai, Na, Nb, obuf):
                                ba = lambda a: a.unsqueeze(2).to_broadcast([128, n, 16])
                                bn = lambda a: a.unsqueeze(1).to_broadcast([128, n, 16])
                                k.op(k.dve, lambda: V.tensor_tensor(out=g1[:, 0:n, :], in0=ba(Ar), in1=bn(Na), op=ALU.mult), [pp_b], [g_b])
                                k.op(k.dve, lambda: V.tensor_tensor(out=g2[:, 0:n, :], in0=ba(Ai), in1=bn(Nb), op=ALU.mult), [pp_b], [g_b])
                                k.op(k.dve, lambda: V.tensor_tensor(out=out3, in0=g1[:, 0:n, :], in1=g2[:, 0:n, :], op=ALU.add), [g_b], [obuf])

                            gen(WT[i][:], 64, ABr[:, gg, :], ABi[:, gg, :], N1[:, gg, :], N2[:, gg, :], WT_b[i])
                            gen(VX[i][:], 65, AFr[:, gg, :], AFi[:, gg, :], C1[:, gg, :], C2[:, gg, :], VX_b[i])
                            gen(L0[i][:], 8, ANr[:, gg, :], ANi[:, gg, :], N1[:, gg, :], N2[:, gg, :], L0_b[i])
                            WTf = WT[i][:].rearrange("p t h -> p (t h)")
                            VXf = VX[i][:].rearrange("p e h -> p (e h)")
                            pst = PS[4][:].bitcast(BF16)
                            k.transposes([(pst[:, jb * 128:(jb + 1) * 128], WTf[:, jb * 128:(jb + 1) * 128], ident[:]) for jb in range(8)], [WT_b[i], cb], [PSB[4]])
                            k.op(k.act, lambda i=i, pst=pst: nc.scalar.activation(out=Wsb[i][:], in_=pst.rearrange("p (jb s) -> p jb s", jb=8), func=AF.Copy), [PSB[4]], [W_b2[i]])
                            k.op(k.act, lambda i=i, pst=pst: nc.scalar.activation(out=Wsw[i][:, :, 0:64], in_=pst.rearrange("p (jb s) -> p jb s", jb=8)[:, :, 64:128], func=AF.Copy), [PSB[4]], [W_b2[i]])
                            k.op(k.act, lambda i=i, pst=pst: nc.scalar.activation(out=Wsw[i][:, :, 64:128], in_=pst.rearrange("p (jb s) -> p jb s", jb=8)[:, :, 0:64], func=AF.Copy), [PSB[4]], [W_b2[i]])
                            k.mm(PS[5][:], [(WTf[:, 896:1024], VXf[:, 16:16 + 512])], [WT_b[i], VX_b[i]], [PSB[5]])
                            k.mm(PS[6][:, 0:384], [(WTf[:, 896:1024], VXf[:, 16 + 512:16 + 896])], [WT_b[i], VX_b[i]], [PSB[6]])
                            k.op(k.act, lambda i=i: nc.scalar.activation(out=Tsb[i][:, 1:5, :], in_=PS[5][:].rearrange("p (d c) -> p d c", d=4), func=AF.Copy), [PSB[5]], [T_b2[i]])
                            k.op(k.act, lambda i=i: nc.scalar.activation(out=Tsb[i][:, 5:8, :], in_=PS[6][:, 0:384].rearrange("p (d c) -> p d c", d=3), func=AF.Copy), [PSB[6]], [T_b2[i]])
                            k.mm(PS[7][:, 0:128], [(L0[i][:].rearrange("p j h -> p (j h)"), VXf[:, 0:128])], [L0_b[i], VX_b[i]], [PSB[7]])
                            k.op(k.dve, lambda: V.tensor_tensor(out=tmpT[:], in0=PS[7][:, 0:128], in1=s5mask[:], op=ALU.mult), [PSB[7], pp_b], [g_b])
                            k.op(k.dve, lambda i=i, gg=gg: V.scalar_tensor_tensor(out=Tsb[i][:, 0, :], in0=identf[:], scalar=DCOL[:, gg:gg + 1], in1=tmpT[:], op0=ALU.mult, op1=ALU.add),
                                 [g_b, pp_b], [T_b2[i]])
                            k.dma(k.sp, [(s5T_d[gg].rearrange("p (d c) -> p d c", d=8), Tsb[i][:]), (s5V_d[gg], VXf)], T_b2[i], [T_b2[i], VX_b[i]], [TV_b[gg], VX_b[i]])
                            k.mm(PS[0][:, 0:NSC], [(Wsb[i][:, jb, :], U_all[:, gg, jb, :]) for jb in range(8)], [W_b2[i], U_b[gg]], [PSB[0]])
                            k.mm(PS[1][:, 0:NSC], [(Wsw[i][:, jb, :], U_all[:, gg, jb, :]) for jb in range(8)], [W_b2[i], U_b[gg]], [PSB[1]])
                            k.op(k.act, lambda gg=gg: nc.scalar.activation(out=Spri[:, gg, :], in_=PS[0][:, 0:NSC], func=AF.Copy), [PSB[0]], [Sp_b])
                            k.op(k.act, lambda gg=gg: nc.scalar.activation(out=Ssec[:, gg, :], in_=PS[1][:, 0:NSC], func=AF.Copy), [PSB[1]], [Sp_b])
                        xa = [sb1(f"s5xa{i}", [128, G], F32) for i in range(2)]
                        xs_ = [sb1(f"s5xs{i}", [128, G], F32) for i in range(2)]
                        w1 = sb1("s5w1", [128, G], F32)
                        w2 = sb1("s5w2", [128, G], F32)
                        sc_b = k.buf()
                        sdv = lambda fn: k.op(k.dve, fn, [sc_b, Sp_b, pp_b], [sc_b, Sp_b])
                        sdv(lambda: V.memset(xa[0][:], 0.0))
                        sdv(lambda: V.memset(xs_[0][:], 0.0))
                        for sc in range(NSC):
                            cu, nx = sc % 2, 1 - sc % 2
                            if sc < NSC - 1:
                                sdv(lambda cu=cu: V.tensor_tensor(out=w1[:], in0=c1[:], in1=xa[cu][:], op=ALU.mult))
                                sdv(lambda cu=cu: V.tensor_tensor(out=w2[:], in0=c2[:], in1=xs_[cu][:], op=ALU.mult))
                                sdv(lambda: V.tensor_tensor(out=w1[:], in0=w1[:], in1=w2[:], op=ALU.add))
                                sdv(lambda nx=nx, sc=sc: V.tensor_tensor(out=xa[nx][:], in0=w1[:], in1=Spri[:, :, sc], op=ALU.add))
                                sdv(lambda cu=cu: V.tensor_tensor(out=w1[:], in0=c1[:], in1=xs_[cu][:], op=ALU.mult))
                                sdv(lambda cu=cu: V.tensor_tensor(out=w2[:], in0=c2[:], in1=xa[cu][:], op=ALU.mult))
                                sdv(lambda: V.tensor_tensor(out=w1[:], in0=w1[:], in1=w2[:], op=ALU.subtract))
                                sdv(lambda nx=nx, sc=sc: V.tensor_tensor(out=xs_[nx][:], in0=w1[:], in1=Ssec[:, :, sc], op=ALU.add))
                            k.op(k.act, lambda cu=cu, sc=sc: nc.scalar.activation(out=Xb[:, :, sc], in_=xa[cu][:], func=AF.Copy), [sc_b], [Xb_b])
                    k.barrier()
                    pA_.close()
                    zT_d = nc.dram_tensor(f"s5Zdram{l}", [512, S], BF16).ap()
                    ZT_b = k.bufs(4)
                    with ExitStack() as p3_:
                        sb3 = lambda n, shp, dt: p3_.enter_context(SBT(n, shp, dt))
                        Tl = [sb3(f"s5Tl{i}", [128, 8, 128], BF16) for i in range(2)]
                        Vl = [sb3(f"s5Vl{i}", [128, 1040], BF16) for i in range(2)]
                        TVl_b = k.bufs(2)
                        Yg = [sb3(f"s5Yg{i}", [128, 8, NSC], F32) for i in range(2)]
                        Yg_b = k.bufs(2)
                        YT = [sb3(f"s5YT{i}", [128, 8, 8 * NSC], F32) for i in range(1)]
                        YT_b = k.bufs(1)
                        zts = sb3("s5zts", [128, 64 * NSC], BF16)
                        zts_b = k.buf()
                        for ct in range(4):
                            yi = 0
                            for gl in range(8):
                                gg = ct * 8 + gl
                                i = gg % 2
                                k.dma(k.sp, [(Tl[i][:], s5T_d[gg].rearrange("p (d c) -> p d c", d=8)), (Vl[i][:], s5V_d[gg])], TVl_b[i], [TV_b[gg]], [TVl_b[i]])
                                for ib in range(8):
                                    pb = ib % 4
                                    pairs = [(Tl[i][:, ib - jb, :], U_all[:, gg, jb, :]) for jb in range(ib + 1)]
                                    pairs.append((Vl[i][:, (8 * ib + 1) * 16:(8 * ib + 9) * 16], Xb[:, gg, :]))
                                    k.mm(PS[pb][:, 0:NSC], pairs, [TVl_b[i], U_b[gg], Xb_b], [PSB[pb]])
                                    k.op(k.act, lambda i=i, ib=ib, pb=pb: nc.scalar.activation(out=Yg[i][:, ib, :], in_=PS[pb][:, 0:NSC], func=AF.Copy), [PSB[pb]], [Yg_b[i]])
                                k.dma(k.sp, [(YT[yi][gl * 16:(gl + 1) * 16, ii, :], Yg[i][ii * 16:(ii + 1) * 16].rearrange("p ib sc -> p (ib sc)")) for ii in range(8)],
                                      Yg_b[i], [Yg_b[i]], [YT_b[yi], Yg_b[i]])
                            k.op(k.act, lambda ct=ct, yi=yi: nc.scalar.activation(out=zts[:], in_=YT[yi][:].rearrange("p i r -> p (i r)"), func=AF.Gelu), [YT_b[yi]], [zts_b])
                            k.dma(k.sp, [(zT_d[ct * 128:(ct + 1) * 128, :], zts[:])], zts_b, [zts_b], [ZT_b[ct]])
                    k.barrier()
                    with ExitStack() as p4_:
                        sb4 = lambda n, shp, dt: p4_.enter_context(SBT(n, shp, dt))
                        wglu = sb4("s5wglu", [128, 4, 512], BF16)
                        wglu_b = k.buf()
                        load_cast([(wglu[:], T["ssm_w_glu"][l].rearrange("(cc p) e -> p cc e", p=128), 4, 512)], wglu_b)
                        zc = [sb4(f"s5zc{i}", [128, 4, 512], BF16) for i in range(2)]
                        zc_b = k.bufs(2)
                        ON = sb4("s5ON", [128, 4, S], BF16)
                        ON_b = k.bufs(4)
                        hb_ = sb4("s5hb", [128, 4], F32)
                        k.op(k.dve, lambda: V.tensor_scalar(out=hb_[:], in0=bglu[:], scalar1=0.5, scalar2=None, op0=ALU.mult), [ld_b, pp_b], [pp_b])
                        th_ = [sb4(f"s5th{i}", [128, 512], F32) for i in range(2)]
                        th_b = k.bufs(2)
                        gts = [sb4(f"s5gt{i}", [128, 4, 512], BF16) for i in range(2)]
                        gts_b = k.bufs(2)
                        ob = [sb4(f"s5ob{i}", [128, 4, 512], BF16) for i in range(2)]
                        ob_b = k.bufs(2)
                        nr_ = 512 // NSC
                        cnt = 0
                        for q in range(S // 512):
                            zi = q % 2
                            k.dma(k.sp, [(zc[zi][:], zT_d.rearrange("(cc p) s -> p cc s", p=128)[:, :, q * 512:(q + 1) * 512])], zc_b[zi], ZT_b, [zc_b[zi]])
                            for ct2 in range(4):
                                onv = ON[:, ct2, :].rearrange("p (sc ib i) -> p i ib sc", ib=8, i=8)
                                pb = cnt % 4
                                ti = cnt % 2
                                cnt += 1
                                k.mm(PS[pb][:], [(wglu[:, cc, ct2 * 128:(ct2 + 1) * 128], zc[zi][:, cc, :]) for cc in range(4)], [wglu_b, zc_b[zi]], [PSB[pb]])
                                k.op(k.act, lambda pb=pb, ti=ti, ct2=ct2: nc.scalar.activation(out=th_[ti][:], in_=PS[pb][:], func=AF.Tanh, scale=0.5, bias=hb_[:, ct2:ct2 + 1]),
                                     [PSB[pb], pp_b], [th_b[ti]])
                                k.op(k.dve, lambda ti=ti: V.tensor_scalar(out=th_[ti][:], in0=th_[ti][:], scalar1=0.5, scalar2=0.5, op0=ALU.mult, op1=ALU.add), [th_b[ti]], [th_b[ti]])
                                r0_ = q * nr_
                                if nr_ >= 8:
                                    ov = onv[:, r0_ // 8:(r0_ + nr_) // 8, :, :]
                                    iv = th_[ti][:].rearrange("p (i ib sc) -> p i ib sc", ib=8, sc=NSC)
                                    zv = zc[zi][:, ct2, :].rearrange("p (i ib sc) -> p i ib sc", ib=8, sc=NSC)
                                else:
                                    ov = onv[:, r0_ // 8, r0_ % 8:r0_ % 8 + nr_, :]
                                    iv = th_[ti][:].rearrange("p (ib sc) -> p ib sc", sc=NSC)
                                    zv = zc[zi][:, ct2, :].rearrange("p (ib sc) -> p ib sc", sc=NSC)
                                k.op(k.dve, lambda ov=ov, iv=iv, zv=zv: V.tensor_tensor(out=ov, in0=iv, in1=zv, op=ALU.mult), [th_b[ti], zc_b[zi]], [ON_b[ct2]])
                        for g in range(NG):
                            i = g % 2
                            k.dma(k.sp, [(gts[i][:], T["gT"][0:512, :].rearrange("(c p) s -> p c s", p=128)[:, :, g * 512:(g + 1) * 512])], gts_b[i], [gT_b[g]], [gts_b[i]])
                            k.op(k.dve, lambda i=i, g=g: V.tensor_tensor(out=ob[i][:], in0=ON[:, :, g * 512:(g + 1) * 512], in1=gts[i][:], op=ALU.mult), ON_b + [gts_b[i]], [ob_b[i]])
                            k.dma(k.sp, [(T["brT"][0:512, :].rearrange("(c p) s -> p c s", p=128)[:, :, g * 512:(g + 1) * 512], ob[i][:])], ob_b[i], [ob_b[i]], [brT_b[0][g]])
                    k.barrier()

            missing = [n for n, nm in enumerate(["s5", "ret", "diff", "mem"]) if nm not in phases]
            if missing:
                with ExitStack() as ps_:
                    zt = ps_.enter_context(SBT("zt", [128, 4, 512], BF16))
                    zt_b = k.buf()
                    k.op(k.dve, lambda: nc.vector.memset(zt[:], 0.0), [], [zt_b])
                    for n in missing:
                        for g in range(NG):
                            k.dma(k.sp, [(T["brT"][n * 512:(n + 1) * 512, :].rearrange("(h p) s -> p h s", p=128)[:, :, g * 512:(g + 1) * 512], zt[:])],
                                  zt_b, [zt_b], [brT_b[n][g]])
                    k.barrier()

            if "merge" in phases:
                with ExitStack() as ps_:
                    sb = lambda n, shp, dt: ps_.enter_context(SBT(n, shp, dt))
                    wm = sb("gwm", [128, 8, 4096], BF16)
                    wb = sb("gwb", [128, 4, 4, 1024], BF16)
                    wo = sb("gwo", [128, 8, 1024], BF16)
                    bm = sb("gbm", [128, 4, 8], F32)
                    w_b = k.buf()
                    bm_b = k.buf()
                    wb_src = T["w_branch"][l].rearrange("n (cc p) d -> p n cc d", p=128)
                    wo_src = T["w_out"][l].rearrange("(kc p) c -> p kc c", p=128)
                    load_cast([(wm[:, :, c * 512:(c + 1) * 512], wv(P_MERGE + c * 512, 512), 8, 512) for c in range(8)]
                              + [(wb[:, n], wb_src[:, n], 4, 1024) for n in range(4)]
                              + [(wo[:, :, c * 512:(c + 1) * 512], wo_src[:, :, c * 512:(c + 1) * 512], 8, 512) for c in range(2)], w_b)
                    load_cols(bm[:].rearrange("p n d -> p (n d)"), 32, [(lambda R: R[0:32, :], T["b_merge"][l].rearrange("n (dt p) -> (n dt) p", p=128))], bm_b)
                    hTs = [sb(f"ghT{i}", [128, 8, 512], BF16) for i in range(2)]
                    hTs_b = k.bufs(2)
                    brs = sb("gbr", [128, 16, 512], BF16)
                    brs_b = k.buf()
                    xg = sb("gx", [128, 4, D], F32)
                    xg_b = k.buf()
                    gate = [sb(f"ggate{i}", [128, 512], BF16) for i in range(2)]
                    gate_b = k.bufs(2)
                    acc = sb("gacc", [128, 512], F32)
                    tmp = sb("gtmp", [128, 512], F32)
                    acc_b, tmp_b = k.bufs(2)
                    mT = sb("gmT", [128, 8, 512], BF16)
                    mT_b = k.buf()
                    for g in range(NG):
                        i = g % 2
                        k.dma(k.sp, [(hTs[i][:], T["hT"].rearrange("(kc p) s -> p kc s", p=128)[:, :, g * 512:(g + 1) * 512])], hTs_b[i], [hT_b[g]], [hTs_b[i]])
                        k.dma(k.sp, [(brs[:], T["brT"].rearrange("(c p) s -> p c s", p=128)[:, :, g * 512:(g + 1) * 512])], brs_b,
                              [brT_b[n][g] for n in range(4)], [brs_b])
                        k.dma(k.sp, [(xg[:], xin[g * 512:(g + 1) * 512, :].rearrange("(t p) d -> p t d", p=128))], xg_b, [xres_b[l][g]], [xg_b])
                        cnt = 0
                        for dt in range(8):
                            for n in range(4):
                                pg = cnt % 2
                                pbk = 2 + cnt % 2
                                gi = cnt % 2
                                cnt += 1
                                k.mm(PS[pg][:], [(wm[:, kc, n * 1024 + dt * 128:n * 1024 + (dt + 1) * 128], hTs[i][:, kc, :]) for kc in range(8)], [w_b, hTs_b[i]], [PSB[pg]])
                                k.op(k.act, lambda pg=pg, gi=gi, n=n, dt=dt: nc.scalar.activation(out=gate[gi][:], in_=PS[pg][:], func=AF.Sigmoid, bias=bm[:, n, dt:dt + 1]),
                                     [PSB[pg], bm_b], [gate_b[gi]])
                                k.mm(PS[pbk][:], [(wb[:, n, cc, dt * 128:(dt + 1) * 128], brs[:, n * 4 + cc, :]) for cc in range(4)], [w_b, brs_b], [PSB[pbk]])
                                if n == 0:
                                    k.op(k.dve, lambda pbk=pbk, gi=gi: nc.vector.tensor_tensor(out=acc[:], in0=PS[pbk][:], in1=gate[gi][:], op=ALU.mult),
                                         [PSB[pbk], gate_b[gi]], [acc_b])
                                else:
                                    k.op(k.dve, lambda pbk=pbk, gi=gi: nc.vector.tensor_tensor(out=tmp[:], in0=PS[pbk][:], in1=gate[gi][:], op=ALU.mult),
                                         [PSB[pbk], gate_b[gi]], [tmp_b])
                                    if n < 3:
                                        k.op(k.dve, lambda: nc.vector.tensor_tensor(out=acc[:], in0=acc[:], in1=tmp[:], op=ALU.add), [acc_b, tmp_b], [acc_b])
                                    else:
                                        k.op(k.dve, lambda dt=dt: nc.vector.tensor_tensor(out=mT[:, dt, :], in0=acc[:], in1=tmp[:], op=ALU.add), [acc_b, tmp_b], [mT_b])
                        for t in range(4):
                            for hf in range(2):
                                pb = 4 + (t * 2 + hf) % 4
                                k.mm(PS[pb][:], [(mT[:, dt, t * 128:(t + 1) * 128], wo[:, dt, hf * 512:(hf + 1) * 512]) for dt in range(8)], [mT_b, w_b], [PSB[pb]])
                                k.op(k.dve, lambda t=t, hf=hf, pb=pb: nc.vector.tensor_tensor(out=xg[:, t, hf * 512:(hf + 1) * 512], in0=PS[pb][:],
                                                                                             in1=xg[:, t, hf * 512:(hf + 1) * 512], op=ALU.add), [PSB[pb], xg_b], [xg_b])
                        k.dma(k.sp, [(xout[g * 512:(g + 1) * 512, :].rearrange("(t p) d -> p t d", p=128), xg[:])], xg_b, [xg_b], [xres_b[l + 1][g]])
                    k.barrier()
        k.finish(xres_b[NL])
    return nc, consts


_CACHE = {}


def run(inputs, S, NL, ncores, debug=False, phases=None, trace=False):
    key = (S, NL, debug, phases)
    if key not in _CACHE:
        kw = {} if phases is None else {"phases": phases}
        _CACHE[key] = build(S, NL, debug, **kw)
    nc, consts = _CACHE[key]
    in_maps = []
    for b in range(ncores):
        m = {"x": np.ascontiguousarray(inputs["x"][b, :S]), "mem": np.ascontiguousarray(inputs["mem"][b])}
        for n in PARAM_NAMES:
            m[n] = np.ascontiguousarray(inputs[n])
        for n, v in consts.items():
            if n != "c_rcdec":
                m[n] = v
        in_maps.append(m)
    res = run_bass_kernel_spmd(nc, in_maps, core_ids=list(range(ncores)), trace=trace)
    return res


def kernel(**inputs):
    inputs = {k_: np.asarray(v) for k_, v in inputs.items()}
    B, S, _ = inputs["x"].shape
    res = run(inputs, S, 2, B)
    return np.stack([np.asarray(r["y"]) for r in res.results], axis=0).astype(np.float32)
```
